# Optimizing a Trainium2 kernel written in Bass

```python
import math
import jax, jax.numpy as jnp
from jax import lax
import numpy as np

D_MODEL = 2048
BATCH = 8
SEQ = 2048
DEPTH = 2

BRANCH = D_MODEL
HEAD_DIM = 128
FOX_HEADS = BRANCH // HEAD_DIM
DIFF_HEADS = BRANCH // (2 * HEAD_DIM)
DIFF_VDIM = 2 * HEAD_DIM
N_MIXERS = 2
N_FOX = (DEPTH + 1) // 2
N_DIFF = DEPTH // 2
FOX_IN = 4 * BRANCH + FOX_HEADS
DIFF_IN = 4 * BRANCH
NUM_BUCKETS = 32
MAX_DISTANCE = 128
Q_BLOCK = 128
ALPHA = (2 * DEPTH) ** 0.25
BETA = (8 * DEPTH) ** -0.25
LN_EPS = 1e-5
RMS_EPS = 1e-5

kernel_name = 'hybrid_fox_diffattn_deepnorm'


def layer_norm(x, g, b):
    xf = x.astype(jnp.float32)
    mu = jnp.mean(xf, axis=-1, keepdims=True)
    xc = xf - mu
    var = jnp.mean(xc * xc, axis=-1, keepdims=True)
    y = xc * lax.rsqrt(var + LN_EPS) * g.astype(jnp.float32) + b.astype(jnp.float32)
    return y.astype(x.dtype)


def causal_mask(q0, kl):
    qpos = q0 + jnp.arange(Q_BLOCK, dtype=jnp.int32)[:, None]
    kpos = jnp.arange(kl, dtype=jnp.int32)[None, :]
    return kpos <= qpos, qpos - kpos


def t5_bucket(rel):
    n = jnp.maximum(rel, 0)
    max_exact = NUM_BUCKETS // 2
    large = max_exact + (jnp.log(jnp.maximum(n, 1).astype(jnp.float32) / max_exact)
                         / math.log(MAX_DISTANCE / max_exact)
                         * (NUM_BUCKETS - max_exact)).astype(jnp.int32)
    large = jnp.minimum(large, NUM_BUCKETS - 1)
    return jnp.where(n < max_exact, n, large)


def fox_mixer(h, w_in, b_f, w_out):
    B, S, _ = h.shape
    proj = h @ w_in
    q, k, v, z, f = jnp.split(proj, [BRANCH, 2 * BRANCH, 3 * BRANCH, 4 * BRANCH], axis=-1)
    q = q.reshape(B, S, FOX_HEADS, HEAD_DIM)
    k = k.reshape(B, S, FOX_HEADS, HEAD_DIM)
    v = v.reshape(B, S, FOX_HEADS, HEAD_DIM)
    logf = jax.nn.log_sigmoid((f + b_f).astype(jnp.float32))
    c = jnp.cumsum(logf, axis=1).transpose(0, 2, 1)
    scale = HEAD_DIM ** -0.5
    outs = []
    for i in range(S // Q_BLOCK):
        q0 = i * Q_BLOCK
        kl = q0 + Q_BLOCK
        s = jnp.einsum('bqhd,bkhd->bhqk', q[:, q0:kl], k[:, :kl],
                       preferred_element_type=jnp.float32) * scale
        s = s + c[:, :, q0:kl, None] - c[:, :, None, :kl]
        mask, _ = causal_mask(q0, kl)
        s = jnp.where(mask, s, -jnp.inf)
        p = jax.nn.softmax(s, axis=-1)
        outs.append(jnp.einsum('bhqk,bkhd->bqhd', p.astype(v.dtype), v[:, :kl]))
    o = jnp.concatenate(outs, axis=1).reshape(B, S, BRANCH)
    return (o * jax.nn.silu(z)) @ w_out


def diff_mixer(h, w_in, lam_q, lam_k, subln_g, w_out, rel_bias, layer_idx):
    B, S, _ = h.shape
    proj = h @ w_in
    q, k, v, z = jnp.split(proj, [BRANCH, 2 * BRANCH, 3 * BRANCH], axis=-1)
    q = q.reshape(B, S, DIFF_HEADS, 2, HEAD_DIM)
    k = k.reshape(B, S, DIFF_HEADS, 2, HEAD_DIM)
    v = v.reshape(B, S, DIFF_HEADS, DIFF_VDIM)
    lambda_init = 0.8 - 0.6 * math.exp(-0.3 * layer_idx)
    lq = lam_q.astype(jnp.float32)
    lk = lam_k.astype(jnp.float32)
    lam = jnp.exp(jnp.sum(lq[0] * lk[0])) - jnp.exp(jnp.sum(lq[1] * lk[1])) + lambda_init
    scale = HEAD_DIM ** -0.5
    table = rel_bias.astype(jnp.float32)
    outs = []
    for i in range(S // Q_BLOCK):
        q0 = i * Q_BLOCK
        kl = q0 + Q_BLOCK
        mask, rel = causal_mask(q0, kl)
        bias = table[t5_bucket(rel)].transpose(2, 0, 1)
        s = jnp.einsum('bqhmd,bkhmd->bhmqk', q[:, q0:kl], k[:, :kl],
                       preferred_element_type=jnp.float32) * scale
        s = s + bias[None, :, None]
        s = jnp.where(mask, s, -jnp.inf)
        p = jax.nn.softmax(s, axis=-1)
        a = p[:, :, 0] - lam * p[:, :, 1]
        outs.append(jnp.einsum('bhqk,bkhe->bqhe', a.astype(v.dtype), v[:, :kl]))
    o = jnp.concatenate(outs, axis=1).astype(jnp.float32)
    o = o * lax.rsqrt(jnp.mean(o * o, axis=-1, keepdims=True) + RMS_EPS) * subln_g.astype(jnp.float32)
    o = (o * (1.0 - lambda_init)).astype(z.dtype).reshape(B, S, BRANCH)
    return (o * jax.nn.silu(z)) @ w_out


def setup_inputs(seed: int = 0) -> dict:
    key = jax.random.key(seed)
    ks = jax.random.split(key, 13)
    f32 = jnp.float32
    x = jax.random.normal(ks[0], (BATCH, SEQ, D_MODEL), f32)
    fox_w_in = jax.random.normal(ks[1], (N_FOX, D_MODEL, FOX_IN), f32) * D_MODEL ** -0.5
    fox_b_f = jax.random.uniform(ks[2], (N_FOX, FOX_HEADS), f32, minval=1.0, maxval=6.0)
    fox_w_out = jax.random.normal(ks[3], (N_FOX, BRANCH, D_MODEL), f32) * (BRANCH ** -0.5) * BETA
    diff_w_in = jax.random.normal(ks[4], (N_DIFF, D_MODEL, DIFF_IN), f32) * D_MODEL ** -0.5
    diff_lam_q = jax.random.normal(ks[5], (N_DIFF, 2, HEAD_DIM), f32) * 0.1
    diff_lam_k = jax.random.normal(ks[6], (N_DIFF, 2, HEAD_DIM), f32) * 0.1
    diff_subln_g = 1.0 + 0.02 * jax.random.normal(ks[7], (N_DIFF, DIFF_VDIM), f32)
    diff_w_out = jax.random.normal(ks[8], (N_DIFF, BRANCH, D_MODEL), f32) * (BRANCH ** -0.5) * BETA
    rel_bias = jax.random.normal(ks[9], (NUM_BUCKETS, DIFF_HEADS), f32) * 0.5
    ln_g = 1.0 + 0.02 * jax.random.normal(ks[10], (DEPTH, D_MODEL), f32)
    ln_b = 0.02 * jax.random.normal(ks[11], (DEPTH, D_MODEL), f32)
    return {'x': x, 'fox_w_in': fox_w_in, 'fox_b_f': fox_b_f, 'fox_w_out': fox_w_out,
            'diff_w_in': diff_w_in, 'diff_lam_q': diff_lam_q, 'diff_lam_k': diff_lam_k,
            'diff_subln_g': diff_subln_g, 'diff_w_out': diff_w_out, 'rel_bias': rel_bias,
            'ln_g': ln_g, 'ln_b': ln_b}


def reference(x, fox_w_in, fox_b_f, fox_w_out, diff_w_in, diff_lam_q, diff_lam_k,
              diff_subln_g, diff_w_out, rel_bias, ln_g, ln_b):
    for i in range(DEPTH):
        j = i // N_MIXERS
        if i % N_MIXERS == 0:
            y = fox_mixer(x, fox_w_in[j], fox_b_f[j], fox_w_out[j])
        else:
            y = diff_mixer(x, diff_w_in[j], diff_lam_q[j], diff_lam_k[j], diff_subln_g[j],
                           diff_w_out[j], rel_bias, i)
        x = layer_norm(ALPHA * x + y, ln_g[i], ln_b[i])
    return x
```

```python
import math
from contextlib import ExitStack

import numpy as np
import concourse.bass as bass
import concourse.mybir as mybir
from concourse.bass_utils import run_bass_kernel_spmd

F32 = mybir.dt.float32
BF16 = mybir.dt.bfloat16
AF = mybir.ActivationFunctionType
ALU = mybir.AluOpType
AX = mybir.AxisListType

S = 2048
D = 2048
NB = 16
KC = 16
DEPTH = 2
ALPHA = (2 * DEPTH) ** 0.25
LN_EPS = 1e-5
RMS_EPS = 1e-5
SCALE = 128 ** -0.5
FOX_IN = 8208
NUM_BUCKETS = 32
MAX_DISTANCE = 128
BIG = 262144.0
EW = 383


class Op:
    __slots__ = ("eng", "fn", "waits", "sig", "idx", "cnt", "dkey", "dval", "hkey")

    def __init__(self, eng, fn, hkey):
        self.eng, self.fn, self.waits, self.sig = eng, fn, [], False
        self.idx, self.cnt, self.dkey, self.dval, self.hkey = -1, 0, None, 0, hkey


class Prog:
    ENG = ("pe", "act", "dve", "pool", "sp")

    def __init__(self, nc, stack):
        self.nc, self.stack = nc, stack
        self.q = {e: [] for e in self.ENG}
        self.regions = {}
        self.dcnt = {}
        self.ndma = 0

    def _dep(self, op, other):
        if other is None or other is op:
            return
        if other.dkey is not None:
            op.waits.append(other)
            return
        if not other.sig:
            lst = self.q[other.eng]
            for k in range(other.idx + 1, len(lst)):
                if lst[k].sig and lst[k].dkey is None and lst[k] is not op:
                    other = lst[k]
                    break
            else:
                other.sig = True
        op.waits.append(other)

    def _access(self, op, reads, writes):
        for (rg, lo, hi) in reads:
            recs = self.regions.setdefault(rg, [])
            for r in recs:
                if r[0] < hi and lo < r[1]:
                    self._dep(op, r[2])
                    r[3][op.hkey] = op
        for (rg, lo, hi) in writes:
            recs = self.regions.setdefault(rg, [])
            new = []
            for r in recs:
                if r[0] < hi and lo < r[1]:
                    w = r[2]
                    if w is not None and (w.hkey != op.hkey or op.hkey != "pe"):
                        self._dep(op, w)
                    for k, rd in r[3].items():
                        if k != op.hkey or op.hkey != "pe":
                            self._dep(op, rd)
                    if r[0] < lo:
                        new.append([r[0], lo, r[2], dict(r[3])])
                    if hi < r[1]:
                        new.append([hi, r[1], r[2], dict(r[3])])
                else:
                    new.append(r)
            new.append([lo, hi, op, {}])
            self.regions[rg] = new

    def op(self, eng, fn, reads=(), writes=(), sig=None):
        o = Op(eng, fn, eng)
        o.idx = len(self.q[eng])
        o.sig = (eng != "pe") if sig is None else sig
        self._access(o, reads, writes)
        self.q[eng].append(o)
        return o

    def dma(self, eng, key, fn, reads=(), writes=()):
        self.ndma += 1
        o = Op(eng, fn, "dma%d" % self.ndma)
        o.idx = len(self.q[eng])
        key = eng + "_" + key
        o.dkey = key
        self.dcnt[key] = self.dcnt.get(key, 0) + 16
        o.dval = self.dcnt[key]
        self._access(o, reads, writes)
        self.q[eng].append(o)
        return o

    def wait_all(self, eng, ops):
        o = Op(eng, None, eng)
        o.idx = len(self.q[eng])
        for x in ops:
            self._dep(o, x)
        self.q[eng].append(o)

    def emit(self):
        nc = self.nc
        sem = {e: self.stack.enter_context(nc.semaphore("ms_" + e)) for e in self.ENG}
        dsem = {k: self.stack.enter_context(nc.semaphore("dq_" + k)) for k in self.dcnt}
        for e in self.ENG:
            c = 0
            for o in self.q[e]:
                if o.sig and o.dkey is None and o.fn is not None:
                    c += 1
                    o.cnt = c
        with nc.Block() as block:
            def mk(ename):
                def body(e):
                    seen = {}
                    for o in self.q[ename]:
                        for w in o.waits:
                            if w.dkey is not None:
                                k, s, v = "d:" + w.dkey, dsem[w.dkey], w.dval
                            else:
                                k, s, v = w.eng, sem[w.eng], w.cnt
                            if seen.get(k, 0) < v:
                                e.wait_ge(s, v)
                                seen[k] = v
                        if o.fn is None:
                            continue
                        ins = o.fn(e)
                        if o.dkey is not None:
                            ins.then_inc(dsem[o.dkey], 16)
                        elif o.sig:
                            ins.then_inc(sem[ename], 1)
                return body
            block.tensor(mk("pe"))
            block.scalar(mk("act"))
            block.vector(mk("dve"))
            block.gpsimd(mk("pool"))
            block.sync(mk("sp"))


def _t5_bucket(n):
    max_exact = NUM_BUCKETS // 2
    if n < max_exact:
        return n
    v = np.log(np.float32(max(n, 1)) / np.float32(max_exact)) / np.float32(math.log(MAX_DISTANCE / max_exact))
    large = max_exact + int(np.float32(v) * np.float32(NUM_BUCKETS - max_exact))
    return min(large, NUM_BUCKETS - 1)


def _consts():
    c = np.zeros((128, 3 * 128 + EW), np.float32)
    c[:, 0:128] = np.eye(128, dtype=np.float32)
    c[:, 128:256] = np.triu(np.ones((128, 128), np.float32))
    c[:, 256:384] = 1.0
    for m in range(127, EW):
        c[_t5_bucket(m - 127), 384 + m] += 1.0
        c[31, 384 + m] -= 1.0
    return c


def build_program(layers):
    nc = bass.Bass("TRN2", target_bir_lowering=False)
    dram_in = lambda n, s: nc.dram_tensor(n, s, F32, kind="ExternalInput").ap()
    x_in = dram_in("x", [S, D])
    consts = dram_in("consts", [128, 3 * 128 + EW])
    ln_g = dram_in("ln_g", [DEPTH, D])
    ln_b = dram_in("ln_b", [DEPTH, D])
    w_in = {}
    w_out = {}
    if 0 in layers:
        w_in[0] = dram_in("w_in0", [D, FOX_IN])
        w_out[0] = dram_in("w_out0", [D, D])
        b_f = dram_in("b_f", [1, 16])
    if 1 in layers:
        w_in[1] = dram_in("w_in1", [D, 4 * D])
        w_out[1] = dram_in("w_out1", [D, D])
        lam_q = dram_in("lam_q", [1, 256])
        lam_k = dram_in("lam_k", [1, 256])
        subln_g = dram_in("subln_g", [1, 256])
        rel_bias = dram_in("rel_bias", [32, 8])
        escr = nc.dram_tensor("escr", [8, 128, EW], F32).ap()
    y_out = nc.dram_tensor("y", [S, D], F32, kind="ExternalOutput").ap()
    x1d = nc.dram_tensor("x1d", [S, D], F32).ap() if len(layers) > 1 else None

    with ExitStack() as st:
        P = Prog(nc, st)
        sb = lambda n, s, d: st.enter_context(nc.sbuf_tensor(n, s, d))
        XA = sb("XA", [128, KC, S], BF16)
        XB = sb("XB", [128, KC, S], BF16)
        W = [sb("W%d" % i, [128, KC, 256], BF16) for i in range(3)]
        UW = 18432
        U = sb("U", [128, UW], BF16)
        LW = 7168
        L = sb("L", [128, LW], BF16)
        ident = sb("ident", [128, 128], BF16)
        tri_bf = sb("tri_bf", [128, 128], BF16)
        nmask_bf = sb("nmask_bf", [128, 128], BF16)
        tbt = sb("tbt", [32, 8], F32)
        gtok = [sb("gtok%d" % i, [128, 2, 256], BF16) for i in range(2)]
        rin = sb("rin", [128, 16], F32)
        lnst = sb("lnst", [128, 32], F32)
        pb = [st.enter_context(nc.psum_tensor("pb%d" % i, [128, 512], F32)) for i in range(8)]

        def rX(buf, kc0, kc1, t0, t1):
            return [(buf, (kc * S + t0) * 2, (kc * S + t1) * 2) for kc in range(kc0, kc1)]
        def rU(lo, hi):
            return [("U", lo * 2, hi * 2)]
        def rL(lo, hi):
            return [("L", lo * 2, hi * 2)]
        def rW(i):
            return [("W%d" % i, 0, KC * 256 * 2)]
        def rPB(i, lo=0, hi=2048):
            return [("pb%d" % i, 0, 2048)]
        def rS(name, lo=0, hi=1 << 20):
            return [(name, lo, hi)]

        qT = U[:, 0:4096].rearrange("p (m t) -> p m t", m=2)
        kT = U[:, 4096:8192].rearrange("p (m t) -> p m t", m=2)
        V4 = U[:, 8192:12320].rearrange("p (b m c) -> p b m c", b=16, m=2)
        V3 = U[:, 8192:12320].rearrange("p (b c) -> p b c", b=16)
        sz = U[:, 12320:16416].rearrange("p (b c) -> p b c", b=16)
        PT_OFF = 16416
        Pt = [U[:, PT_OFF + s * 256: PT_OFF + (s + 1) * 256] for s in range(4)]
        def r_qT(m, t0, t1): return rU(m * 2048 + t0, m * 2048 + t1)
        def r_kT(m, t0, t1): return rU(4096 + m * 2048 + t0, 4096 + m * 2048 + t1)
        def r_V(b0, b1): return rU(8192 + b0 * 258, 8192 + b1 * 258)
        def r_sz(b0, b1): return rU(12320 + b0 * 256, 12320 + b1 * 256)
        def r_Pt(s): return rU(PT_OFF + s * 256, PT_OFF + (s + 1) * 256)
        lng = U[:, 0:4096].bitcast(F32)
        lnb = U[:, 4096:8192].bitcast(F32)
        xres = [U[:, 8192:12288].bitcast(F32), U[:, 12288:16384].bitcast(F32)]
        xbf = U[:, 16384:18432]
        r_lng, r_lnb = rU(0, 4096), rU(4096, 8192)
        r_xres = [rU(8192, 12288), rU(12288, 16384)]
        r_xbf = rU(16384, 18432)
        xb0 = [U[:, 8192 + k * 2048:8192 + (k + 1) * 2048] for k in range(4)]
        r_xb0 = [rU(8192 + k * 2048, 8192 + (k + 1) * 2048) for k in range(4)]

        c_id = P.dma("pool", "c0", lambda e: e.dma_start(out=ident[:], in_=consts[:, 0:128]),
                     writes=rS("ident"))
        P.dma("pool", "c1", lambda e: e.dma_start(out=tri_bf[:], in_=consts[:, 128:256]), writes=rS("tri_bf"))
        P.op("pool", lambda e: e.memset(lnst[:, 10:12], -0.5), writes=rS("lnst", 10, 12))
        P.op("dve", lambda e: e.tensor_scalar(out=nmask_bf[:], in0=tri_bf[:], scalar1=-1.0, scalar2=BIG, op0=ALU.add, op1=ALU.mult),
             reads=rS("tri_bf"), writes=rS("nmask_bf"))

        pb_bf = [pb[i][:, :].bitcast(BF16) for i in range(8)]

        carry = {"fn": None}

        def prologue(Xbuf, Xname, after_block=None):
            for b in range(NB):
                s = b % 4
                P.dma("pool", "xl%d" % s, lambda e, b=b, s=s: e.dma_start(out=xb0[s], in_=x_in[b * 128:(b + 1) * 128, :]),
                      writes=r_xb0[s])
                banks = (4, 5) if b % 2 == 0 else (6, 7)
                def tr(e, b=b, s=s, banks=banks):
                    ins = None
                    for c in range(KC):
                        o = pb_bf[banks[c // 8]][:, (c % 8) * 128:(c % 8 + 1) * 128]
                        ins = e.transpose(o, xb0[s][:, c * 128:(c + 1) * 128], ident[:])
                    return ins
                P.op("pe", tr, reads=r_xb0[s] + rS("ident"), writes=rPB(banks[0]) + rPB(banks[1]))
                for hf in range(2):
                    eng = "dve" if hf == 0 else "act"
                    src = pb_bf[banks[hf]].rearrange("p (c t) -> p c t", c=8)
                    dst = Xbuf[:, hf * 8:(hf + 1) * 8, b * 128:(b + 1) * 128]
                    if eng == "dve":
                        fn = lambda e, src=src, dst=dst: e.tensor_copy(out=dst, in_=src)
                    else:
                        fn = lambda e, src=src, dst=dst: e.activation(out=dst, in_=src, func=AF.Copy)
                    P.op(eng, fn, reads=rPB(banks[hf]), writes=rX(Xname, hf * 8, hf * 8 + 8, b * 128, (b + 1) * 128))
                if after_block is not None:
                    after_block(b)

        def make_layer(li, Xbuf, Xname, Gbuf, Gname, resid_src, resid_reg, out_dst, out_reg, make_next):
            fox = (li == 0)
            ph = {}
            win = w_in[li]
            winv = win.rearrange("(kc p) n -> p kc n", p=128)
            wov = w_out[li].rearrange("(kc p) n -> p kc n", p=128)
            lambda_init = 0.8 - 0.6 * math.exp(-0.3 * li)

            wstate = {"n": 0}
            def load_w(col0, after=None):
                slot = wstate["n"] % 3
                wstate["n"] += 1
                P.dma("pool", "w%d" % slot,
                      lambda e: e.dma_start(out=W[slot][:], in_=winv[:, :, col0:col0 + 256]),
                      reads=(rW(after) if after is not None else []), writes=rW(slot))
                return slot

            if fox:
                wf = L[:, 0:256].rearrange("p (k h) -> p k h", k=16)
                bfb = L[:, 256:288].bitcast(F32)
                spf = L[:, 288:800].bitcast(F32)
                spv = spf.rearrange("p (b h) -> p b h", b=16)
                tot = L[:, 800:1312].bitcast(F32).rearrange("p (b h) -> p b h", b=16)
                cneg = L[:, 1312:1824].bitcast(F32).rearrange("p (b h) -> p b h", b=16)
                crefn = L[:, 1824:2336].bitcast(F32).rearrange("p (b h) -> p b h", b=16)
                bias2 = [L[:, 2336 + k * 1024: 2336 + (k + 1) * 1024].bitcast(F32).rearrange("p (m j i) -> p m j i", m=2, j=16)
                         for k in range(2)]
                tri_f = L[:, 4384:4640].bitcast(F32)
                ones_f = L[:, 4640:4896].bitcast(F32)
                ph["fdone"] = set()
                def load_wf():
                    if "wf" not in ph["fdone"]:
                        ph["fdone"].add("wf")
                        P.dma("pool", "c2", lambda e: e.dma_start(out=wf, in_=winv[:, :, 8192:8208]), writes=rL(0, 256))
                def fproj_block(b):
                    load_wf()
                    ph["fdone"].add(b)
                    def g(e):
                        ins = None
                        for kc in range(KC):
                            ins = e.matmul(pb[0][:, b * 16:(b + 1) * 16], lhsT=Xbuf[:, kc, b * 128:(b + 1) * 128],
                                           rhs=wf[:, kc, :], start=(kc == 0), stop=(kc == KC - 1))
                        return ins
                    P.op("pe", g, reads=rX(Xname, 0, KC, b * 128, (b + 1) * 128) + rL(0, 256), writes=rPB(0))
                ph["load_wf"] = load_wf
                ph["fproj_block"] = fproj_block
                P.dma("sp", "c3", lambda e: e.dma_start(out=bfb, in_=b_f.partition_broadcast(128)), writes=rL(256, 288))
                P.dma("sp", "c4", lambda e: e.dma_start(out=tri_f, in_=consts[:, 128:256]), writes=rL(4384, 4640))
                P.dma("sp", "c5", lambda e: e.dma_start(out=ones_f, in_=consts[:, 256:384]), writes=rL(4640, 4896))
            else:
                Bt = L[:, 0:4096].rearrange("p (h k l q) -> p h k l q", h=8, k=2, l=2)
                gain = L[:, 4096:4608].bitcast(F32)
                misc = L[:, 4608:4640].bitcast(F32)
                o0n = L[:, 4640:5664].bitcast(F32).rearrange("p (i c) -> p i c", i=2)
                lqb = L[:, 4640:5152].bitcast(F32)
                lkb = L[:, 5152:5664].bitcast(F32)
                etmp = L[:, 4640:5152].bitcast(F32).rearrange("p (k q) -> p k q", k=2)
                lh = L[:, 5152:5408].bitcast(F32)
                oh = L[:, 5664:6430].bitcast(F32)
                sqj = L[:, 6432:6944].bitcast(F32)
                prod = L[:, 5664:6176].bitcast(F32)
                erep = ptmp_o[:, :, :].rearrange("p i c -> p (i c)")[:, 0:EW]
                tb = tbt
                R_GAIN, R_MISC, R_LQ, R_LK, R_ETMP, R_LH, R_OH, R_PROD = (rL(4096, 4608), rL(4608, 4640), rL(4640, 5152), rL(5152, 5664),
                                                                          rL(4640, 5152), rL(5152, 5408), rL(5664, 6430), rL(5664, 6176))
                P.dma("sp", "c2", lambda e: e.dma_start(out=lqb, in_=lam_q.partition_broadcast(128)), writes=R_LQ)
                P.dma("sp", "c3", lambda e: e.dma_start(out=lkb, in_=lam_k.partition_broadcast(128)), writes=R_LK)
                P.dma("sp", "c4", lambda e: e.dma_start(out=gain, in_=subln_g.partition_broadcast(128)), writes=R_GAIN)
                P.dma("sp", "c5", lambda e: e.dma_start(out=tb[:, :], in_=rel_bias), writes=rS("tbt"))

            ph["slots0"] = []
            def init_load(i, chain=False):
                prev = ph["slots0"][-1] if (chain and ph["slots0"]) else None
                ph["slots0"].append(load_w(2048 * i, prev))
            ph["init_load"] = init_load
            ph["early"] = set()

            def setup_fox():
                for b in range(NB):
                    if b not in ph["fdone"]:
                        ph["fproj_block"](b)
                pf = pb[0][:, 0:256].rearrange("p (b h) -> p b h", b=16)
                P.op("dve", lambda e: e.tensor_tensor(out=spv, in0=pf, in1=bfb.unsqueeze(1).to_broadcast([128, 16, 16]), op=ALU.add),
                     reads=rPB(0, 0, 1024) + rL(256, 288), writes=rL(288, 800))
                P.op("act", lambda e: e.activation(out=spf, in_=spf, func=AF.Exp, scale=-1.0), reads=rL(288, 800), writes=rL(288, 800))
                P.op("act", lambda e: e.activation(out=spf, in_=spf, func=AF.Ln, bias=1.0), reads=rL(288, 800), writes=rL(288, 800))
                ph["fox_part2"] = setup_fox2

            def setup_fox2():
                pf = pb[0][:, 0:256].rearrange("p (b h) -> p b h", b=16)
                P.op("pe", lambda e: e.matmul(pb[1][:, 0:256], lhsT=ones_f, rhs=spf, start=True, stop=True),
                     reads=rL(288, 800) + rL(4640, 4896), writes=rPB(1, 0, 1024))
                P.op("pe", lambda e: e.matmul(pb[2][:, 0:256], lhsT=tri_f, rhs=spf, start=True, stop=True),
                     reads=rL(288, 800) + rL(4384, 4640), writes=rPB(2))
                P.op("dve", lambda e: e.tensor_copy(out=tot, in_=pb[1][:, 0:256].rearrange("p (b h) -> p b h", b=16)),
                     reads=rPB(1, 0, 1024), writes=rL(800, 1312))
                P.op("dve", lambda e: e.tensor_copy(out=crefn[:, 0, :], in_=tot[:, 0, :]), reads=rL(800, 832), writes=rL(1824, 1856))
                for b in range(1, NB):
                    P.op("dve", lambda e, b=b: e.tensor_tensor(out=crefn[:, b, :], in0=crefn[:, b - 1, :], in1=tot[:, b, :], op=ALU.add),
                         reads=rL(1824 + (b - 1) * 32, 1824 + b * 32) + rL(800 + b * 32, 800 + (b + 1) * 32),
                         writes=rL(1824 + b * 32, 1824 + (b + 1) * 32))
                pwin = pb[2][:, 0:256].rearrange("p (b h) -> p b h", b=16)
                P.op("dve", lambda e: e.tensor_copy(out=cneg[:, 0, :], in_=pwin[:, 0, :]), reads=rPB(2), writes=rL(1312, 1344))
                P.op("dve", lambda e: e.tensor_tensor(out=cneg[:, 1:16, :], in0=pwin[:, 1:16, :], in1=crefn[:, 0:15, :], op=ALU.add),
                     reads=rPB(2) + rL(1824, 2336), writes=rL(1344, 1824))
            neglam = None
            r_neglam = []
            if not fox:
                P.op("dve", lambda e: e.tensor_tensor(out=prod, in0=lqb, in1=lkb, op=ALU.mult), reads=R_LQ + R_LK, writes=R_PROD)
                P.op("dve", lambda e: e.reduce_sum(out=misc[:, 0:2], in_=prod.rearrange("p (k d) -> p k d", k=2), axis=AX.X),
                     reads=R_PROD, writes=rL(4608, 4612))
                P.op("act", lambda e: e.activation(out=misc[:, 2:4], in_=misc[:, 0:2], func=AF.Exp), reads=rL(4608, 4612), writes=rL(4612, 4616))
                P.op("dve", lambda e: e.tensor_tensor(out=misc[:, 4:5], in0=misc[:, 3:4], in1=misc[:, 2:3], op=ALU.subtract),
                     reads=rL(4612, 4616), writes=rL(4616, 4618))
                P.op("dve", lambda e: e.tensor_single_scalar(out=misc[:, 5:6], in_=misc[:, 4:5], scalar=-lambda_init, op=ALU.add),
                     reads=rL(4616, 4618), writes=rL(4618, 4620))
                neglam = misc[:, 5:6]
                r_neglam = rL(4618, 4620)
                P.op("dve", lambda e: e.tensor_single_scalar(out=gain, in_=gain, scalar=(1.0 - lambda_init) * 16.0, op=ALU.mult),
                     reads=R_GAIN, writes=R_GAIN)
                P.dma("sp", "c6", lambda e: e.dma_start(out=oh[0:32, :], in_=consts[0:32, 384:384 + EW]), writes=R_OH)

            def setup_head_pre(h):
                P.op("dve", lambda e: e.tensor_copy(out=lh[0:32, :], in_=tb[0:32, h:h + 1].to_broadcast([32, 128])),
                     reads=rS("tbt"), writes=R_LH)

            def setup_head_a(h):
                P.op("pe", lambda e: e.matmul(pb[6][:, 0:EW], lhsT=lh[0:32, :], rhs=oh[0:32, :], start=True, stop=True),
                     reads=R_LH + R_OH, writes=rPB(6))
                P.op("act", lambda e: e.activation(out=erep, in_=pb[6][:, 0:EW], func=AF.Copy, scale=1.0 / SCALE), reads=rPB(6), writes=rS("ov"))
                P.dma("sp", "es", lambda e: e.dma_start(out=escr[h], in_=erep), reads=rS("ov"), writes=rS("escr", h, h + 1))
                for k in range(2):
                    srcap = bass.AP(escr.tensor, h * 128 * EW + 127 + 128 * k, [[EW - 1, 128], [1, 128]])
                    P.dma("sp", "el%d" % k, lambda e, k=k, srcap=srcap: e.dma_start(out=etmp[:, k, :], in_=srcap),
                          reads=rS("escr", h, h + 1), writes=rL(4640 + k * 256, 4640 + (k + 1) * 256))

            def setup_head_b(h):
                r0, r1 = rL(4640, 4896), rL(4896, 5152)
                P.op("dve", lambda e: e.tensor_tensor(out=etmp[:, 0, :], in0=etmp[:, 0, :], in1=tri_bf[:], op=ALU.mult),
                     reads=r0 + rS("tri_bf"), writes=r0)
                P.op("dve", lambda e: e.tensor_tensor(out=etmp[:, 0, :], in0=etmp[:, 0, :], in1=nmask_bf[:], op=ALU.add),
                     reads=r0 + rS("nmask_bf"), writes=r0)
                for k, rk in ((0, r0), (1, r1)):
                    base = (h * 4 + k * 2) * 128
                    P.op("dve", lambda e, k=k: e.tensor_copy(out=Bt[:, h, k, 0, :], in_=etmp[:, k, :]),
                         reads=rk, writes=rL(base, base + 128))
                    P.op("dve", lambda e, k=k: e.tensor_tensor(out=Bt[:, h, k, 1, :], in0=etmp[:, k, :], in1=Bt[:, h, k, 0, :], op=ALU.subtract),
                         reads=rk + rL(base, base + 128), writes=rL(base + 128, base + 256))

            def setup_head(h):
                setup_head_pre(h)
                setup_head_a(h)
                setup_head_b(h)

            deferred = []
            clock = {"t": 0}
            bank_rr = {"n": 0}
            trbank = {"n": 0}

            seqc = {"p": 0}

            def defer(n, fn, tag="bc", p=0):
                deferred.append([clock["t"] + n, fn, tag, p])

            def flush_tag(tag, pmax=1 << 30):
                k = 0
                while k < len(deferred):
                    if deferred[k][2] == tag and deferred[k][3] <= pmax:
                        deferred.pop(k)[1]()
                    else:
                        k += 1

            def tick(flush=False):
                clock["t"] += 1
                k = 0
                while k < len(deferred):
                    if flush or deferred[k][0] <= clock["t"]:
                        fn = deferred.pop(k)[1]
                        fn()
                    else:
                        k += 1

            def acc_ap(m, k):
                w = 129 if fox else 257
                bk = 3 + 2 * m + k
                return pb[bk][:, 0:w], rPB(bk)

            def early_proj(tc):
                sq, sk = ph["slots0"][0], ph["slots0"][1]
                proj_fm_tc(sq, qT, r_qT, "act", tc)
                proj_fm_tc(sk, kT, r_kT, "dve", tc)
                ph["early"].add(tc)
            ph["early_proj"] = early_proj

            def proj_fm_tc(slot, dstT, r_dst, eng, tc):
                if tc == 3 and carry["fn"] is not None:
                    fn_, carry["fn"] = carry["fn"], None
                    fn_()
                if True:
                    for m in range(2):
                        if True:
                            bk = bank_rr["n"] % 4
                            bank_rr["n"] += 1
                            def g(e, m=m, tc=tc, bk=bk):
                                ins = None
                                for kc in range(KC):
                                    ins = e.matmul(pb[bk][:, :], lhsT=W[slot][:, kc, m * 128:(m + 1) * 128],
                                                   rhs=Xbuf[:, kc, tc * 512:(tc + 1) * 512], start=(kc == 0), stop=(kc == KC - 1))
                                return ins
                            P.op("pe", g, reads=rW(slot) + rX(Xname, 0, KC, tc * 512, (tc + 1) * 512), writes=rPB(bk))
                            if eng == "act":
                                fn = lambda e, m=m, tc=tc, bk=bk: e.activation(out=dstT[:, m, tc * 512:(tc + 1) * 512], in_=pb[bk][:, :], func=AF.Copy)
                            else:
                                fn = lambda e, m=m, tc=tc, bk=bk: e.tensor_copy(out=dstT[:, m, tc * 512:(tc + 1) * 512], in_=pb[bk][:, :])
                            P.op(eng, fn, reads=rPB(bk), writes=r_dst(m, tc * 512, (tc + 1) * 512))
                            tick()

            def do_unit(u, slots):
                sq, sk, sv = slots
                def proj_fm(slot, dstT, r_dst, eng):
                    for tc in range(4):
                        if u == 0 and tc in ph["early"]:
                            continue
                        proj_fm_tc(slot, dstT, r_dst, eng, tc)

                def proj_tm(slot, evac):
                    for bp in range(8):
                        bk = bank_rr["n"] % 4
                        bank_rr["n"] += 1
                        def g(e, bp=bp, bk=bk):
                            ins = None
                            for bb in range(2):
                                b = 2 * bp + bb
                                for kc in range(KC):
                                    ins = e.matmul(pb[bk][:, bb * 256:(bb + 1) * 256], lhsT=Xbuf[:, kc, b * 128:(b + 1) * 128],
                                                   rhs=W[slot][:, kc, :], start=(kc == 0), stop=(kc == KC - 1))
                            return ins
                        P.op("pe", g, reads=rW(slot) + rX(Xname, 0, KC, bp * 256, (bp + 1) * 256), writes=rPB(bk))
                        evac(bp, bk)
                        tick()

                proj_fm(sq, qT, r_qT, "act")
                if fox and u == 0:
                    ph["fox_part2"]()
                sz_slot = load_w(6144 + 256 * u)
                proj_fm(sk, kT, r_kT, "dve")
                nxt = []
                if u + 1 < 8:
                    nxt.append(load_w(256 * (u + 1)))

                def evac_v(bp, bk):
                    if fox:
                        src = pb[bk][:, :].rearrange("p (b m c) -> p b m c", b=2, m=2)
                        dst = V4[:, 2 * bp:2 * bp + 2, :, 0:128]
                    else:
                        src = pb[bk][:, :].rearrange("p (b c) -> p b c", b=2)
                        dst = V3[:, 2 * bp:2 * bp + 2, 0:256]
                    P.op("dve", lambda e: e.tensor_copy(out=dst, in_=src), reads=rPB(bk), writes=r_V(2 * bp, 2 * bp + 2))
                proj_tm(sv, evac_v)
                if u + 1 < 8:
                    nxt.append(load_w(2048 + 256 * (u + 1)))

                def evac_z(bp, bk):
                    src = pb[bk][:, :].rearrange("p (b c) -> p b c", b=2)
                    P.op("act", lambda e: e.activation(out=sz[:, 2 * bp:2 * bp + 2, :], in_=src, func=AF.Silu),
                         reads=rPB(bk), writes=r_sz(2 * bp, 2 * bp + 2))
                proj_tm(sz_slot, evac_z)
                if u + 1 < 8:
                    nxt.append(load_w(4096 + 256 * (u + 1)))
                else:
                    for q4 in range(4):
                        P.dma("pool", "wo%d" % q4,
                              lambda e, q4=q4: e.dma_start(out=Xbuf[:, 4 * q4:4 * q4 + 4, :], in_=wov[:, 4 * q4:4 * q4 + 4, :]),
                              writes=rX(Xname, 4 * q4, 4 * q4 + 4, 0, S))

                if fox:
                    bsel = bias2[u % 2]
                    boff = 2336 + (u % 2) * 1024
                    for m in range(2):
                        h = 2 * u + m
                        P.op("dve", lambda e, m=m, h=h: e.tensor_tensor(
                            out=bsel[:, m, :, :], in0=cneg[:, :, h:h + 1].to_broadcast([128, 16, 16]),
                            in1=crefn[:, :, h].unsqueeze(1).to_broadcast([128, 16, 16]), op=ALU.subtract),
                            reads=rL(1312, 2336), writes=rL(boff + m * 512, boff + (m + 1) * 512))
                tiles = []
                pmap = {}
                for I in (0, 7, 1, 6, 2, 5, 3, 4):
                    pmap[I] = seqc["p"]
                    seqc["p"] += 1
                    for m in range(2):
                        for j in range(2 * I + 2):
                            tiles.append((I, m, j))
                T = len(tiles)

                def geom(t):
                    I, m, j = tiles[t]
                    s = t % 4
                    i0 = 2 * I if j <= 2 * I else 2 * I + 1
                    nblk = 2 * I + 2 - i0
                    return I, m, j, s, i0, nblk, t % 3, 0

                def qk(t):
                    I, m, j, s, i0, nblk, bank, off = geom(t)
                    w = nblk * 128
                    q0 = i0 * 128
                    extra = []
                    for k in range(nblk):
                        d = i0 + k - j
                        if fox and d == 0:
                            extra.append((k, nmask_bf[:], rS("nmask_bf")))
                        elif (not fox) and d in (0, 1):
                            for l in range(2):
                                base = (u * 4 + d * 2 + l) * 128
                                extra.append((k, Bt[:, u, d, l, :], rL(base, base + 128)))
                    def g(e):
                        ins = e.matmul(pb[bank][:, 0:w], lhsT=kT[:, m, j * 128:(j + 1) * 128], rhs=qT[:, m, q0:q0 + w],
                                       start=True, stop=(len(extra) == 0))
                        for n, (k, rhs, _) in enumerate(extra):
                            ins = e.matmul(pb[bank][:, k * 128:(k + 1) * 128], lhsT=ident[:], rhs=rhs, start=False, stop=(n == len(extra) - 1))
                        return ins
                    rd = r_kT(m, j * 128, (j + 1) * 128) + r_qT(m, q0, q0 + w)
                    for _, _, r in extra:
                        rd = rd + r
                    if extra:
                        rd = rd + rS("ident")
                    P.op("pe", g, reads=rd, writes=rPB(bank), sig=True)

                def r_Pblk(s, k):
                    return rU(PT_OFF + s * 256 + k * 128, PT_OFF + s * 256 + (k + 1) * 128)

                def ex(t):
                    I, m, j, s, i0, nblk, bank, off = geom(t)
                    w = nblk * 128
                    src = pb[bank][:, 0:w]
                    if fox:
                        P.op("act", lambda e: e.activation(out=Pt[s][:, 0:w], in_=src, func=AF.Exp,
                                                           bias=bsel[:, m, j, 2 * I:2 * I + 1], scale=SCALE),
                             reads=rPB(bank) + rL(boff + m * 512, boff + (m + 1) * 512),
                             writes=rU(PT_OFF + s * 256, PT_OFF + s * 256 + w))
                    else:
                        P.op("act", lambda e: e.activation(out=Pt[s][:, 0:w], in_=src, func=AF.Exp, scale=SCALE),
                             reads=rPB(bank), writes=rU(PT_OFF + s * 256, PT_OFF + s * 256 + w))

                def pv(t):
                    I, m, j, s, i0, nblk, bank, off = geom(t)
                    for k in range(nblk):
                        i = i0 + k
                        out, r_out = acc_ap(m, i - 2 * I)
                        rhs = V4[:, j, m, :] if fox else V3[:, j, 0:257]
                        P.op("pe", lambda e, k=k, out=out, rhs=rhs, i=i: e.matmul(out, lhsT=Pt[s][:, k * 128:(k + 1) * 128], rhs=rhs,
                                                                                   start=(j == 0), stop=(j == i)),
                             reads=r_Pblk(s, k) + r_V(j, j + 1), writes=r_out)
                    if j == 2 * I + 1:
                        finalize(I, m)

                def emit_transposes(I, gt, gname, c0, ncol):
                    tb_ = 7
                    def tr(e):
                        ins = None
                        for c in range(ncol):
                            for k in range(2):
                                ins = e.transpose(pb_bf[tb_][:, (c * 2 + k) * 128:(c * 2 + k + 1) * 128],
                                                  gt[:, k, (c0 % 2 + c) * 128:(c0 % 2 + c + 1) * 128] if fox else gt[:, k, c * 128:(c + 1) * 128],
                                                  ident[:])
                        return ins
                    if fox:
                        rd = [(gname, (k * 256 + (c0 % 2) * 128) * 2, (k * 256 + (c0 % 2 + 1) * 128) * 2) for k in range(2)]
                    else:
                        rd = rS(gname)
                    P.op("pe", tr, reads=rd + rS("ident"), writes=rPB(tb_, 0, ncol * 512))
                    P.op("dve", lambda e: e.tensor_copy(out=Gbuf[:, c0:c0 + ncol, 256 * I:256 * I + 256],
                                                        in_=pb_bf[tb_][:, 0:ncol * 256].rearrange("p (c t) -> p c t", c=ncol)),
                         reads=rPB(tb_, 0, ncol * 512), writes=rX(Gname, c0, c0 + ncol, 256 * I, 256 * I + 256))

                def finalize(I, m):
                    p = pmap[I]
                    gt = gtok[p % 2]
                    gname = "gtok%d" % (p % 2)
                    if fox:
                        flush_tag("tr", p - 2)
                        ro = (p % 2) * 8 + m * 2
                        for k in range(2):
                            blk = 2 * I + k
                            a, r_a = acc_ap(m, k)
                            P.op("dve", lambda e, k=k, a=a: e.reciprocal(out=rin[:, ro + k:ro + k + 1], in_=a[:, 128:129]),
                                 reads=r_a, writes=rS("rin", ro + k, ro + k + 1))
                            P.op("dve", lambda e, k=k, a=a, blk=blk: e.scalar_tensor_tensor(
                                out=gt[:, k, m * 128:(m + 1) * 128], in0=a[:, 0:128], scalar=rin[:, ro + k:ro + k + 1],
                                in1=sz[:, blk, m * 128:(m + 1) * 128], op0=ALU.mult, op1=ALU.mult),
                                reads=r_a + rS("rin", ro + k, ro + k + 1) + r_sz(blk, blk + 1),
                                writes=rS(gname, (k * 256 + m * 128) * 2, (k * 256 + (m + 1) * 128) * 2))
                        defer(10, lambda: emit_transposes(I, gt, gname, 2 * u + m, 1), "tr", p)
                    else:
                        ro = (p % 2) * 8
                        ov = ptmp_o
                        if m == 0:
                            for k in range(2):
                                a, r_a = acc_ap(0, k)
                                P.op("dve", lambda e, k=k, a=a: e.reciprocal(out=rin[:, ro + k:ro + k + 1], in_=a[:, 256:257]),
                                     reads=r_a, writes=rS("rin", ro + k, ro + k + 1))
                                P.op("dve", lambda e, k=k, a=a: e.tensor_scalar_mul(out=o0n[:, k, :], in0=a[:, 0:256], scalar1=rin[:, ro + k:ro + k + 1]),
                                     reads=r_a + rS("rin", ro + k, ro + k + 1), writes=rL(4640 + k * 512, 4640 + (k + 1) * 512))
                        else:
                            flush_tag("bc")
                            for k in range(2):
                                a, r_a = acc_ap(1, k)
                                P.op("dve", lambda e, k=k, a=a: e.reciprocal(out=rin[:, ro + 2 + k:ro + 3 + k], in_=a[:, 256:257]),
                                     reads=r_a, writes=rS("rin", ro + 2 + k, ro + 3 + k))
                                P.op("dve", lambda e, k=k, a=a: e.tensor_copy(out=ov[:, k, :], in_=a[:, 0:256]),
                                     reads=r_a, writes=rS("ov", k * 1024, (k + 1) * 1024))
                            for k in range(2):
                                P.op("dve", lambda e, k=k: e.tensor_tensor(out=rin[:, ro + 4 + k:ro + 5 + k], in0=rin[:, ro + 2 + k:ro + 3 + k],
                                                                          in1=neglam, op=ALU.mult),
                                     reads=rS("rin", ro + 2 + k, ro + 3 + k) + r_neglam, writes=rS("rin", ro + 4 + k, ro + 5 + k))
                                P.op("dve", lambda e, k=k: e.scalar_tensor_tensor(out=ov[:, k, :], in0=ov[:, k, :], scalar=rin[:, ro + 4 + k:ro + 5 + k],
                                                                                  in1=o0n[:, k, :], op0=ALU.mult, op1=ALU.add),
                                     reads=rS("ov", k * 1024, (k + 1) * 1024) + rS("rin", ro + 4 + k, ro + 5 + k) + rL(4640 + k * 512, 4640 + (k + 1) * 512),
                                     writes=rS("ov", k * 1024, (k + 1) * 1024))
                            for k in range(2):
                                P.op("dve", lambda e, k=k: e.scalar_tensor_tensor(out=sqj[:, :], in0=ov[:, k, :], scalar=1.0, in1=ov[:, k, :],
                                                                                  op0=ALU.mult, op1=ALU.mult,
                                                                                  accum_out=rin[:, ro + 6 + k:ro + 7 + k]),
                                     reads=rS("ov", k * 1024, (k + 1) * 1024), writes=rL(6432, 6944) + rS("rin", ro + 6 + k, ro + 7 + k))
                            P.op("pool", lambda e: e.tensor_single_scalar(out=rin[:, ro + 6:ro + 8], in_=rin[:, ro + 6:ro + 8],
                                                                          scalar=256.0 * RMS_EPS, op=ALU.add),
                                 reads=rS("rin", ro + 6, ro + 8), writes=rS("rin", ro + 6, ro + 8))
                            P.op("pool", lambda e: e.tensor_tensor(out=rin[:, ro + 6:ro + 8], in0=rin[:, ro + 6:ro + 8], in1=lnst[:, 10:12], op=ALU.pow),
                                 reads=rS("rin", ro + 6, ro + 8) + rS("lnst", 10, 12), writes=rS("rin", ro + 6, ro + 8))
                            def stage_c():
                                flush_tag("tr", p - 2)
                                for k in range(2):
                                    blk = 2 * I + k
                                    P.op("dve", lambda e, k=k: e.scalar_tensor_tensor(out=ov[:, k, :], in0=ov[:, k, :], scalar=rin[:, ro + 6 + k:ro + 7 + k],
                                                                                      in1=gain, op0=ALU.mult, op1=ALU.mult),
                                         reads=rS("ov", k * 1024, (k + 1) * 1024) + rS("rin", ro + 6 + k, ro + 7 + k) + R_GAIN,
                                         writes=rS("ov", k * 1024, (k + 1) * 1024))
                                    P.op("dve", lambda e, k=k, blk=blk: e.tensor_tensor(out=gt[:, k, :], in0=ov[:, k, :], in1=sz[:, blk, :], op=ALU.mult),
                                         reads=rS("ov", k * 1024, (k + 1) * 1024) + r_sz(blk, blk + 1), writes=rS(gname, k * 512, (k + 1) * 512))
                            defer(13, stage_c, "bc", p)
                            defer(30, lambda: emit_transposes(I, gt, gname, 2 * u, 2), "tr", p)

                for t in range(T + 2):
                    if t < T:
                        qk(t)
                    if 0 <= t - 1 < T:
                        ex(t - 1)
                    if t - 2 >= 0:
                        pv(t - 2)
                    tick()
                return nxt

            def units():
                if fox:
                    P.op("pool", lambda e: e.memset(V4[:, :, :, 128:129], 1.0), writes=r_V(0, 16))
                else:
                    P.op("pool", lambda e: e.memset(V3[:, :, 256:257], 1.0), writes=r_V(0, 16))
                slots = ph["slots0"]
                for u in range(8):
                    slots = do_unit(u, slots)
                tick(flush=True)

            def epilogue(per_block=None, pre_block=None):
                Wo = Xbuf
                P.dma("sp", "lg", lambda e: e.dma_start(out=lng, in_=ln_g[li:li + 1, :].partition_broadcast(128)), writes=r_lng)
                P.dma("sp", "lb", lambda e: e.dma_start(out=lnb, in_=ln_b[li:li + 1, :].partition_broadcast(128)), writes=r_lnb)
                out_dmas = []

                def load_res(b):
                    s = b % 2
                    rd = [(resid_reg, b, b + 1)] if resid_reg else []
                    P.dma("sp", "xr%d" % s, lambda e: e.dma_start(out=xres[s], in_=resid_src[b * 128:(b + 1) * 128, :]),
                          reads=rd, writes=r_xres[s])

                def epi_y(b):
                    s = b % 2
                    xr = xres[s]
                    so = s * 16
                    for p4 in range(4):
                        def g(e, p4=p4):
                            ins = None
                            for kc in range(KC):
                                ins = e.matmul(pb[p4][:, :], lhsT=Gbuf[:, kc, b * 128:(b + 1) * 128], rhs=Wo[:, kc, p4 * 512:(p4 + 1) * 512],
                                               start=(kc == 0), stop=(kc == KC - 1))
                            return ins
                        P.op("pe", g, reads=rX(Gname, 0, KC, b * 128, (b + 1) * 128) + rX(Xname, 0, KC, p4 * 512, (p4 + 1) * 512), writes=rPB(p4))
                        lo, hi = 8192 + s * 4096 + p4 * 1024, 8192 + s * 4096 + (p4 + 1) * 1024
                        P.op("dve", lambda e, p4=p4: e.scalar_tensor_tensor(out=xr[:, p4 * 512:(p4 + 1) * 512], in0=xr[:, p4 * 512:(p4 + 1) * 512],
                                                                            scalar=float(ALPHA), in1=pb[p4][:, :], op0=ALU.mult, op1=ALU.add,
                                                                            accum_out=lnst[:, so + p4:so + p4 + 1]),
                             reads=rU(lo, hi) + rPB(p4), writes=rU(lo, hi) + rS("lnst", so + p4, so + p4 + 1))
                    if b + 1 < NB:
                        load_res(b + 1)

                def epi_ln(b):
                    s = b % 2
                    xr = xres[s]
                    so = s * 16
                    c = lambda k: lnst[:, so + k:so + k + 1]
                    rc = lambda k0, k1: rS("lnst", so + k0, so + k1)
                    P.op("act", lambda e: e.activation(out=xbf, in_=xr, func=AF.Square, accum_out=c(6)),
                         reads=r_xres[s], writes=r_xbf + rc(6, 7))
                    P.op("dve", lambda e: e.reduce_sum(out=c(4), in_=lnst[:, so:so + 4], axis=AX.X), reads=rc(0, 4), writes=rc(4, 5))
                    P.op("dve", lambda e: e.tensor_single_scalar(out=c(5), in_=c(4), scalar=-1.0 / D, op=ALU.mult), reads=rc(4, 5), writes=rc(5, 6))
                    P.op("dve", lambda e: e.tensor_tensor(out=c(7), in0=c(5), in1=c(5), op=ALU.mult), reads=rc(5, 6), writes=rc(7, 8))
                    P.op("dve", lambda e: e.scalar_tensor_tensor(out=c(8), in0=c(6), scalar=1.0 / D, in1=c(7), op0=ALU.mult, op1=ALU.subtract),
                         reads=rc(6, 8), writes=rc(8, 9))
                    P.op("act", lambda e: e.activation(out=c(9), in_=c(8), func=AF.Ln, bias=LN_EPS), reads=rc(8, 9), writes=rc(9, 10))
                    P.op("act", lambda e: e.activation(out=c(9), in_=c(9), func=AF.Exp, scale=-0.5), reads=rc(9, 10), writes=rc(9, 10))
                    P.op("dve", lambda e: e.scalar_tensor_tensor(out=xr, in0=xr, scalar=c(5), in1=lng, op0=ALU.add, op1=ALU.mult),
                         reads=r_xres[s] + rc(5, 6) + r_lng, writes=r_xres[s])
                    P.op("dve", lambda e: e.scalar_tensor_tensor(out=xr, in0=xr, scalar=c(9), in1=lnb, op0=ALU.mult, op1=ALU.add),
                         reads=r_xres[s] + rc(9, 10) + r_lnb, writes=r_xres[s])
                    wr = [(out_reg, b, b + 1)] if out_reg else []
                    out_dmas.append(P.dma("sp", "xo%d" % s, lambda e: e.dma_start(out=out_dst[b * 128:(b + 1) * 128, :], in_=xr),
                                          reads=r_xres[s], writes=wr))
                    if make_next:
                        P.op("act", lambda e: e.activation(out=xbf, in_=xr, func=AF.Copy), reads=r_xres[s], writes=r_xbf)

                def epi_tr(b):
                    banks = (4, 5)
                    def tr(e):
                        ins = None
                        for c in range(KC):
                            o = pb_bf[banks[c // 8]][:, (c % 8) * 128:(c % 8 + 1) * 128]
                            ins = e.transpose(o, xbf[:, c * 128:(c + 1) * 128], ident[:])
                        return ins
                    P.op("pe", tr, reads=r_xbf + rS("ident"), writes=rPB(banks[0]) + rPB(banks[1]))
                    for hf in range(2):
                        src = pb_bf[banks[hf]].rearrange("p (c t) -> p c t", c=8)
                        dst = Gbuf[:, hf * 8:(hf + 1) * 8, b * 128:(b + 1) * 128]
                        if hf == 0:
                            fn = lambda e, src=src, dst=dst: e.tensor_copy(out=dst, in_=src)
                        else:
                            fn = lambda e, src=src, dst=dst: e.activation(out=dst, in_=src, func=AF.Copy)
                        P.op("dve" if hf == 0 else "act", fn, reads=rPB(banks[hf]),
                             writes=rX(Gname, hf * 8, hf * 8 + 8, b * 128, (b + 1) * 128))

                load_res(0)
                for b in range(NB):
                    if pre_block is not None:
                        pre_block(b)
                    epi_y(b)
                    if make_next and b >= 1:
                        epi_tr(b - 1)
                    if per_block is not None:
                        per_block(b)
                    epi_ln(b)
                if make_next:
                    carry["fn"] = lambda: epi_tr(NB - 1)
                return out_dmas

            ph["setup_fox"] = setup_fox if fox else (lambda: None)
            ph["setup_head"] = (lambda h: None) if fox else setup_head
            ph["setup_head_a"] = (lambda h: None) if fox else setup_head_a
            ph["setup_head_pre"] = (lambda h: None) if fox else setup_head_pre
            ph["setup_head_b"] = (lambda h: None) if fox else setup_head_b
            ph["units"] = units
            ph["epilogue"] = epilogue
            return ph

        ptmp_o = sb("ov", [128, 2, 256], F32)

        bufs = [(XA, "XA", XB, "XB"), (XB, "XB", XA, "XA")]
        specs = []
        for n, li in enumerate(layers):
            Xb, Xn, Gb, Gn = bufs[n % 2]
            first = (n == 0)
            final = (n == len(layers) - 1)
            specs.append((li, Xb, Xn, Gb, Gn, x_in if first else x1d, None if first else "x1d",
                          y_out if final else x1d, None if final else "x1d", not final))
        last = []
        cur = make_layer(*specs[0])
        def pro_hook(b, cur=cur):
            if specs[0][0] != 0:
                return
            if b == 3:
                cur["load_wf"]()
            if b >= 8:
                cur["fproj_block"](2 * (b - 8))
                cur["fproj_block"](2 * (b - 8) + 1)
        prologue(XA, "XA", pro_hook)
        for i in range(3):
            cur["init_load"](i, chain=True)
        if specs[0][0] == 1:
            for h in range(8):
                cur["setup_head"](h)
        for n in range(len(specs)):
            cur["setup_fox"]()
            cur["units"]()
            nxt = None
            hook = None
            if n + 1 < len(specs):
                nxt = make_layer(*specs[n + 1])
                for i in range(3):
                    nxt["init_load"](i)
                def hook(b, nxt=nxt):
                    if 1 <= b <= 8:
                        nxt["setup_head_b"](b - 1)
                    if b < 8:
                        nxt["setup_head_a"](b)
            pre = (lambda b, nxt=nxt: nxt["setup_head_pre"](b) if b < 8 else None) if nxt is not None else None
            last = cur["epilogue"](hook, pre)
            cur = nxt
        P.wait_all("sp", last)
        P.emit()
    return nc


_CONSTS = None


def _in_map(layers, xb, inp):
    global _CONSTS
    if _CONSTS is None:
        _CONSTS = _consts()
    m = {"x": np.ascontiguousarray(xb, dtype=np.float32), "consts": _CONSTS,
         "ln_g": np.ascontiguousarray(inp["ln_g"], dtype=np.float32), "ln_b": np.ascontiguousarray(inp["ln_b"], dtype=np.float32)}
    if 0 in layers:
        m["w_in0"] = np.ascontiguousarray(inp["fox_w_in"][0]); m["w_out0"] = np.ascontiguousarray(inp["fox_w_out"][0])
        m["b_f"] = np.ascontiguousarray(inp["fox_b_f"][0:1])
    if 1 in layers:
        m["w_in1"] = np.ascontiguousarray(inp["diff_w_in"][0]); m["w_out1"] = np.ascontiguousarray(inp["diff_w_out"][0])
        m["lam_q"] = np.ascontiguousarray(inp["diff_lam_q"][0].reshape(1, 256)); m["lam_k"] = np.ascontiguousarray(inp["diff_lam_k"][0].reshape(1, 256))
        m["subln_g"] = np.ascontiguousarray(inp["diff_subln_g"][0:1]); m["rel_bias"] = np.ascontiguousarray(inp["rel_bias"])
    return m


_PROGS = {}


def run_layers(layers, x, inp, trace=False):
    key = tuple(layers)
    if key not in _PROGS:
        _PROGS[key] = build_program(list(layers))
    nc = _PROGS[key]
    in_maps = [_in_map(layers, x[b], inp) for b in range(8)]
    res = run_bass_kernel_spmd(nc, in_maps, core_ids=list(range(8)))
    return np.stack([r["y"] for r in res.results], axis=0)


MODE = "fused"


def kernel(x, fox_w_in, fox_b_f, fox_w_out, diff_w_in, diff_lam_q, diff_lam_k, diff_subln_g, diff_w_out,
           rel_bias, ln_g, ln_b):
    inp = dict(fox_w_in=np.asarray(fox_w_in), fox_b_f=np.asarray(fox_b_f), fox_w_out=np.asarray(fox_w_out),
               diff_w_in=np.asarray(diff_w_in), diff_lam_q=np.asarray(diff_lam_q), diff_lam_k=np.asarray(diff_lam_k),
               diff_subln_g=np.asarray(diff_subln_g), diff_w_out=np.asarray(diff_w_out), rel_bias=np.asarray(rel_bias),
               ln_g=np.asarray(ln_g), ln_b=np.asarray(ln_b))
    x = np.asarray(x, dtype=np.float32)
    if MODE == "fused":
        return run_layers([0, 1], x, inp).astype(np.float32)
    x1 = run_layers([0], x, inp)
    return run_layers([1], x1, inp).astype(np.float32)
```

```python
import math
from contextlib import ExitStack

import numpy as np
import concourse.bass as bass
import concourse.mybir as mybir
from concourse.bass_utils import run_bass_kernel_spmd

F32 = mybir.dt.float32
BF16 = mybir.dt.bfloat16
AF = mybir.ActivationFunctionType
ALU = mybir.AluOpType
AX = mybir.AxisListType

S = 2048
D = 2048
NB = 16
KC = 16
DEPTH = 2
ALPHA = (2 * DEPTH) ** 0.25
LN_EPS = 1e-5
RMS_EPS = 1e-5
SCALE = 128 ** -0.5
FOX_IN = 8208
NUM_BUCKETS = 32
MAX_DISTANCE = 128
BIG = 262144.0
EW = 383


class Op:
    __slots__ = ("eng", "fn", "waits", "sig", "idx", "cnt", "dkey", "dval", "hkey")

    def __init__(self, eng, fn, hkey):
        self.eng, self.fn, self.waits, self.sig = eng, fn, [], False
        self.idx, self.cnt, self.dkey, self.dval, self.hkey = -1, 0, None, 0, hkey


class Prog:
    ENG = ("pe", "act", "dve", "pool", "sp")

    def __init__(self, nc, stack):
        self.nc, self.stack = nc, stack
        self.q = {e: [] for e in self.ENG}
        self.regions = {}
        self.dcnt = {}
        self.ndma = 0

    def _dep(self, op, other):
        if other is None or other is op:
            return
        if other.dkey is not None:
            op.waits.append(other)
            return
        if not other.sig:
            lst = self.q[other.eng]
            for k in range(other.idx + 1, len(lst)):
                if lst[k].sig and lst[k].dkey is None and lst[k] is not op:
                    other = lst[k]
                    break
            else:
                other.sig = True
        op.waits.append(other)

    def _access(self, op, reads, writes):
        for (rg, lo, hi) in reads:
            recs = self.regions.setdefault(rg, [])
            for r in recs:
                if r[0] < hi and lo < r[1]:
                    self._dep(op, r[2])
                    r[3][op.hkey] = op
        for (rg, lo, hi) in writes:
            recs = self.regions.setdefault(rg, [])
            new = []
            for r in recs:
                if r[0] < hi and lo < r[1]:
                    w = r[2]
                    if w is not None and (w.hkey != op.hkey or op.hkey != "pe"):
                        self._dep(op, w)
                    for k, rd in r[3].items():
                        if k != op.hkey or op.hkey != "pe":
                            self._dep(op, rd)
                    if r[0] < lo:
                        new.append([r[0], lo, r[2], dict(r[3])])
                    if hi < r[1]:
                        new.append([hi, r[1], r[2], dict(r[3])])
                else:
                    new.append(r)
            new.append([lo, hi, op, {}])
            self.regions[rg] = new

    def op(self, eng, fn, reads=(), writes=(), sig=None):
        o = Op(eng, fn, eng)
        o.idx = len(self.q[eng])
        o.sig = (eng != "pe") if sig is None else sig
        self._access(o, reads, writes)
        self.q[eng].append(o)
        return o

    def dma(self, eng, key, fn, reads=(), writes=()):
        self.ndma += 1
        o = Op(eng, fn, "dma%d" % self.ndma)
        o.idx = len(self.q[eng])
        key = eng + "_" + key
        o.dkey = key
        self.dcnt[key] = self.dcnt.get(key, 0) + 16
        o.dval = self.dcnt[key]
        self._access(o, reads, writes)
        self.q[eng].append(o)
        return o

    def wait_all(self, eng, ops):
        o = Op(eng, None, eng)
        o.idx = len(self.q[eng])
        for x in ops:
            self._dep(o, x)
        self.q[eng].append(o)

    def emit(self):
        nc = self.nc
        sem = {e: self.stack.enter_context(nc.semaphore("ms_" + e)) for e in self.ENG}
        dsem = {k: self.stack.enter_context(nc.semaphore("dq_" + k)) for k in self.dcnt}
        for e in self.ENG:
            c = 0
            for o in self.q[e]:
                if o.sig and o.dkey is None and o.fn is not None:
                    c += 1
                    o.cnt = c
        with nc.Block() as block:
            def mk(ename):
                def body(e):
                    seen = {}
                    for o in self.q[ename]:
                        for w in o.waits:
                            if w.dkey is not None:
                                k, s, v = "d:" + w.dkey, dsem[w.dkey], w.dval
                            else:
                                k, s, v = w.eng, sem[w.eng], w.cnt
                            if seen.get(k, 0) < v:
                                e.wait_ge(s, v)
                                seen[k] = v
                        if o.fn is None:
                            continue
                        ins = o.fn(e)
                        if o.dkey is not None:
                            ins.then_inc(dsem[o.dkey], 16)
                        elif o.sig:
                            ins.then_inc(sem[ename], 1)
                return body
            block.tensor(mk("pe"))
            block.scalar(mk("act"))
            block.vector(mk("dve"))
            block.gpsimd(mk("pool"))
            block.sync(mk("sp"))


def _t5_bucket(n):
    max_exact = NUM_BUCKETS // 2
    if n < max_exact:
        return n
    v = np.log(np.float32(max(n, 1)) / np.float32(max_exact)) / np.float32(math.log(MAX_DISTANCE / max_exact))
    large = max_exact + int(np.float32(v) * np.float32(NUM_BUCKETS - max_exact))
    return min(large, NUM_BUCKETS - 1)


def _consts():
    c = np.zeros((128, 3 * 128 + EW), np.float32)
    c[:, 0:128] = np.eye(128, dtype=np.float32)
    c[:, 128:256] = np.triu(np.ones((128, 128), np.float32))
    c[:, 256:384] = 1.0
    for m in range(127, EW):
        c[_t5_bucket(m - 127), 384 + m] += 1.0
        c[31, 384 + m] -= 1.0
    return c


def build_program(layers):
    nc = bass.Bass("TRN2", target_bir_lowering=False)
    dram_in = lambda n, s: nc.dram_tensor(n, s, F32, kind="ExternalInput").ap()
    x_in = dram_in("x", [S, D])
    consts = dram_in("consts", [128, 3 * 128 + EW])
    ln_g = dram_in("ln_g", [DEPTH, D])
    ln_b = dram_in("ln_b", [DEPTH, D])
    w_in = {}
    w_out = {}
    if 0 in layers:
        w_in[0] = dram_in("w_in0", [D, FOX_IN])
        w_out[0] = dram_in("w_out0", [D, D])
        b_f = dram_in("b_f", [1, 16])
    if 1 in layers:
        w_in[1] = dram_in("w_in1", [D, 4 * D])
        w_out[1] = dram_in("w_out1", [D, D])
        lam_q = dram_in("lam_q", [1, 256])
        lam_k = dram_in("lam_k", [1, 256])
        subln_g = dram_in("subln_g", [1, 256])
        rel_bias = dram_in("rel_bias", [32, 8])
        escr = nc.dram_tensor("escr", [8, 128, EW], F32).ap()
    y_out = nc.dram_tensor("y", [S, D], F32, kind="ExternalOutput").ap()
    x1d = nc.dram_tensor("x1d", [S, D], F32).ap() if len(layers) > 1 else None

    with ExitStack() as st:
        P = Prog(nc, st)
        sb = lambda n, s, d: st.enter_context(nc.sbuf_tensor(n, s, d))
        XA = sb("XA", [128, KC, S], BF16)
        XB = sb("XB", [128, KC, S], BF16)
        W = [sb("W%d" % i, [128, KC, 256], BF16) for i in range(3)]
        UW = 18432
        U = sb("U", [128, UW], BF16)
        LW = 7168
        L = sb("L", [128, LW], BF16)
        ident = sb("ident", [128, 128], BF16)
        tri_bf = sb("tri_bf", [128, 128], BF16)
        nmask_bf = sb("nmask_bf", [128, 128], BF16)
        tbt = sb("tbt", [32, 8], F32)
        gtok = [sb("gtok%d" % i, [128, 2, 256], BF16) for i in range(2)]
        rin = sb("rin", [128, 16], F32)
        lnst = sb("lnst", [128, 32], F32)
        pb = [st.enter_context(nc.psum_tensor("pb%d" % i, [128, 512], F32)) for i in range(8)]

        def rX(buf, kc0, kc1, t0, t1):
            return [(buf, (kc * S + t0) * 2, (kc * S + t1) * 2) for kc in range(kc0, kc1)]
        def rU(lo, hi):
            return [("U", lo * 2, hi * 2)]
        def rL(lo, hi):
            return [("L", lo * 2, hi * 2)]
        def rW(i):
            return [("W%d" % i, 0, KC * 256 * 2)]
        def rPB(i, lo=0, hi=2048):
            return [("pb%d" % i, 0, 2048)]
        def rS(name, lo=0, hi=1 << 20):
            return [(name, lo, hi)]

        qT = U[:, 0:4096].rearrange("p (m t) -> p m t", m=2)
        kT = U[:, 4096:8192].rearrange("p (m t) -> p m t", m=2)
        V4 = U[:, 8192:12320].rearrange("p (b m c) -> p b m c", b=16, m=2)
        V3 = U[:, 8192:12320].rearrange("p (b c) -> p b c", b=16)
        sz = U[:, 12320:16416].rearrange("p (b c) -> p b c", b=16)
        PT_OFF = 16416
        Pt = [U[:, PT_OFF + s * 256: PT_OFF + (s + 1) * 256] for s in range(4)]
        def r_qT(m, t0, t1): return rU(m * 2048 + t0, m * 2048 + t1)
        def r_kT(m, t0, t1): return rU(4096 + m * 2048 + t0, 4096 + m * 2048 + t1)
        def r_V(b0, b1): return rU(8192 + b0 * 258, 8192 + b1 * 258)
        def r_sz(b0, b1): return rU(12320 + b0 * 256, 12320 + b1 * 256)
        def r_Pt(s): return rU(PT_OFF + s * 256, PT_OFF + (s + 1) * 256)
        lng = U[:, 0:4096].bitcast(F32)
        lnb = U[:, 4096:8192].bitcast(F32)
        xres = [U[:, 8192:12288].bitcast(F32), U[:, 12288:16384].bitcast(F32)]
        xbf = U[:, 16384:18432]
        r_lng, r_lnb = rU(0, 4096), rU(4096, 8192)
        r_xres = [rU(8192, 12288), rU(12288, 16384)]
        r_xbf = rU(16384, 18432)
        xb0 = [U[:, 8192 + k * 2048:8192 + (k + 1) * 2048] for k in range(4)]
        r_xb0 = [rU(8192 + k * 2048, 8192 + (k + 1) * 2048) for k in range(4)]

        c_id = P.dma("pool", "c0", lambda e: e.dma_start(out=ident[:], in_=consts[:, 0:128]),
                     writes=rS("ident"))
        P.dma("pool", "c1", lambda e: e.dma_start(out=tri_bf[:], in_=consts[:, 128:256]), writes=rS("tri_bf"))
        P.op("pool", lambda e: e.memset(lnst[:, 10:12], -0.5), writes=rS("lnst", 10, 12))
        P.op("dve", lambda e: e.tensor_scalar(out=nmask_bf[:], in0=tri_bf[:], scalar1=-1.0, scalar2=BIG, op0=ALU.add, op1=ALU.mult),
             reads=rS("tri_bf"), writes=rS("nmask_bf"))

        pb_bf = [pb[i][:, :].bitcast(BF16) for i in range(8)]

        carry = {"fn": None}

        def prologue(Xbuf, Xname, after_block=None):
            for b in range(NB):
                s = b % 4
                P.dma("pool", "xl%d" % s, lambda e, b=b, s=s: e.dma_start(out=xb0[s], in_=x_in[b * 128:(b + 1) * 128, :]),
                      writes=r_xb0[s])
                banks = (4, 5) if b % 2 == 0 else (6, 7)
                def tr(e, b=b, s=s, banks=banks):
                    ins = None
                    for c in range(KC):
                        o = pb_bf[banks[c // 8]][:, (c % 8) * 128:(c % 8 + 1) * 128]
                        ins = e.transpose(o, xb0[s][:, c * 128:(c + 1) * 128], ident[:])
                    return ins
                P.op("pe", tr, reads=r_xb0[s] + rS("ident"), writes=rPB(banks[0]) + rPB(banks[1]))
                for hf in range(2):
                    eng = "dve" if hf == 0 else "act"
                    src = pb_bf[banks[hf]].rearrange("p (c t) -> p c t", c=8)
                    dst = Xbuf[:, hf * 8:(hf + 1) * 8, b * 128:(b + 1) * 128]
                    if eng == "dve":
                        fn = lambda e, src=src, dst=dst: e.tensor_copy(out=dst, in_=src)
                    else:
                        fn = lambda e, src=src, dst=dst: e.activation(out=dst, in_=src, func=AF.Copy)
                    P.op(eng, fn, reads=rPB(banks[hf]), writes=rX(Xname, hf * 8, hf * 8 + 8, b * 128, (b + 1) * 128))
                if after_block is not None:
                    after_block(b)

        def make_layer(li, Xbuf, Xname, Gbuf, Gname, resid_src, resid_reg, out_dst, out_reg, make_next):
            fox = (li == 0)
            ph = {}
            win = w_in[li]
            winv = win.rearrange("(kc p) n -> p kc n", p=128)
            wov = w_out[li].rearrange("(kc p) n -> p kc n", p=128)
            lambda_init = 0.8 - 0.6 * math.exp(-0.3 * li)

            wstate = {"n": 0}
            def load_w(col0, split=1):
                slot = wstate["n"] % 3
                wstate["n"] += 1
                if split == 1:
                    P.dma("pool", "w%d" % slot,
                          lambda e: e.dma_start(out=W[slot][:], in_=winv[:, :, col0:col0 + 256]), writes=rW(slot))
                else:
                    per = KC // split
                    for q in range(split):
                        P.dma("pool", "w%dp%d" % (slot, q),
                              lambda e, q=q: e.dma_start(out=W[slot][:, q * per:(q + 1) * per, :],
                                                         in_=winv[:, q * per:(q + 1) * per, col0:col0 + 256]),
                              writes=[("W%d" % slot, q * per * 512, (q + 1) * per * 512)])
                return slot

            if fox:
                wf = L[:, 0:256].rearrange("p (k h) -> p k h", k=16)
                bfb = L[:, 256:288].bitcast(F32)
                spf = L[:, 288:800].bitcast(F32)
                spv = spf.rearrange("p (b h) -> p b h", b=16)
                tot = L[:, 800:1312].bitcast(F32).rearrange("p (b h) -> p b h", b=16)
                cneg = L[:, 1312:1824].bitcast(F32).rearrange("p (b h) -> p b h", b=16)
                crefn = L[:, 1824:2336].bitcast(F32).rearrange("p (b h) -> p b h", b=16)
                bias2 = [L[:, 2336 + k * 1024: 2336 + (k + 1) * 1024].bitcast(F32).rearrange("p (m j i) -> p m j i", m=2, j=16)
                         for k in range(2)]
                tri_f = L[:, 4384:4640].bitcast(F32)
                ones_f = L[:, 4640:4896].bitcast(F32)
                ph["fdone"] = set()
                def load_wf():
                    if "wf" not in ph["fdone"]:
                        ph["fdone"].add("wf")
                        P.dma("pool", "c2", lambda e: e.dma_start(out=wf, in_=winv[:, :, 8192:8208]), writes=rL(0, 256))
                def fproj_block(b):
                    load_wf()
                    ph["fdone"].add(b)
                    def g(e):
                        ins = None
                        for kc in range(KC):
                            ins = e.matmul(pb[0][:, b * 16:(b + 1) * 16], lhsT=Xbuf[:, kc, b * 128:(b + 1) * 128],
                                           rhs=wf[:, kc, :], start=(kc == 0), stop=(kc == KC - 1))
                        return ins
                    P.op("pe", g, reads=rX(Xname, 0, KC, b * 128, (b + 1) * 128) + rL(0, 256), writes=rPB(0))
                ph["load_wf"] = load_wf
                ph["fproj_block"] = fproj_block
                P.dma("sp", "c3", lambda e: e.dma_start(out=bfb, in_=b_f.partition_broadcast(128)), writes=rL(256, 288))
                P.dma("sp", "c4", lambda e: e.dma_start(out=tri_f, in_=consts[:, 128:256]), writes=rL(4384, 4640))
                P.dma("sp", "c5", lambda e: e.dma_start(out=ones_f, in_=consts[:, 256:384]), writes=rL(4640, 4896))
            else:
                Bt = L[:, 0:4096].rearrange("p (h k l q) -> p h k l q", h=8, k=2, l=2)
                gain = L[:, 4096:4608].bitcast(F32)
                misc = L[:, 4608:4640].bitcast(F32)
                o0n = L[:, 4640:5664].bitcast(F32).rearrange("p (i c) -> p i c", i=2)
                lqb = L[:, 4640:5152].bitcast(F32)
                lkb = L[:, 5152:5664].bitcast(F32)
                etmp = L[:, 4640:5152].bitcast(F32).rearrange("p (k q) -> p k q", k=2)
                lh = L[:, 5152:5408].bitcast(F32)
                oh = L[:, 5664:6430].bitcast(F32)
                sqj = L[:, 6432:6944].bitcast(F32)
                prod = L[:, 5664:6176].bitcast(F32)
                erep = ptmp_o[:, :, :].rearrange("p i c -> p (i c)")[:, 0:EW]
                tb = tbt
                R_GAIN, R_MISC, R_LQ, R_LK, R_ETMP, R_LH, R_OH, R_PROD = (rL(4096, 4608), rL(4608, 4640), rL(4640, 5152), rL(5152, 5664),
                                                                          rL(4640, 5152), rL(5152, 5408), rL(5664, 6430), rL(5664, 6176))
                P.dma("sp", "c2", lambda e: e.dma_start(out=lqb, in_=lam_q.partition_broadcast(128)), writes=R_LQ)
                P.dma("sp", "c3", lambda e: e.dma_start(out=lkb, in_=lam_k.partition_broadcast(128)), writes=R_LK)
                P.dma("sp", "c4", lambda e: e.dma_start(out=gain, in_=subln_g.partition_broadcast(128)), writes=R_GAIN)
                P.dma("sp", "c5", lambda e: e.dma_start(out=tb[:, :], in_=rel_bias), writes=rS("tbt"))

            ph["slots0"] = []
            def init_load(i, split=1):
                ph["slots0"].append(load_w(2048 * i, split))
            ph["init_load"] = init_load
            ph["early"] = set()

            def setup_fox():
                for b in range(NB):
                    if b not in ph["fdone"]:
                        ph["fproj_block"](b)
                pf = pb[0][:, 0:256].rearrange("p (b h) -> p b h", b=16)
                P.op("dve", lambda e: e.tensor_tensor(out=spv, in0=pf, in1=bfb.unsqueeze(1).to_broadcast([128, 16, 16]), op=ALU.add),
                     reads=rPB(0, 0, 1024) + rL(256, 288), writes=rL(288, 800))
                P.op("act", lambda e: e.activation(out=spf, in_=spf, func=AF.Exp, scale=-1.0), reads=rL(288, 800), writes=rL(288, 800))
                P.op("act", lambda e: e.activation(out=spf, in_=spf, func=AF.Ln, bias=1.0), reads=rL(288, 800), writes=rL(288, 800))
                ph["fox_part2"] = setup_fox2

            def setup_fox2():
                pf = pb[0][:, 0:256].rearrange("p (b h) -> p b h", b=16)
                P.op("pe", lambda e: e.matmul(pb[1][:, 0:256], lhsT=ones_f, rhs=spf, start=True, stop=True),
                     reads=rL(288, 800) + rL(4640, 4896), writes=rPB(1, 0, 1024))
                P.op("pe", lambda e: e.matmul(pb[2][:, 0:256], lhsT=tri_f, rhs=spf, start=True, stop=True),
                     reads=rL(288, 800) + rL(4384, 4640), writes=rPB(2))
                P.op("dve", lambda e: e.tensor_copy(out=tot, in_=pb[1][:, 0:256].rearrange("p (b h) -> p b h", b=16)),
                     reads=rPB(1, 0, 1024), writes=rL(800, 1312))
                P.op("dve", lambda e: e.tensor_copy(out=crefn[:, 0, :], in_=tot[:, 0, :]), reads=rL(800, 832), writes=rL(1824, 1856))
                for b in range(1, NB):
                    P.op("dve", lambda e, b=b: e.tensor_tensor(out=crefn[:, b, :], in0=crefn[:, b - 1, :], in1=tot[:, b, :], op=ALU.add),
                         reads=rL(1824 + (b - 1) * 32, 1824 + b * 32) + rL(800 + b * 32, 800 + (b + 1) * 32),
                         writes=rL(1824 + b * 32, 1824 + (b + 1) * 32))
                pwin = pb[2][:, 0:256].rearrange("p (b h) -> p b h", b=16)
                P.op("dve", lambda e: e.tensor_copy(out=cneg[:, 0, :], in_=pwin[:, 0, :]), reads=rPB(2), writes=rL(1312, 1344))
                P.op("dve", lambda e: e.tensor_tensor(out=cneg[:, 1:16, :], in0=pwin[:, 1:16, :], in1=crefn[:, 0:15, :], op=ALU.add),
                     reads=rPB(2) + rL(1824, 2336), writes=rL(1344, 1824))
            neglam = None
            r_neglam = []
            if not fox:
                P.op("dve", lambda e: e.tensor_tensor(out=prod, in0=lqb, in1=lkb, op=ALU.mult), reads=R_LQ + R_LK, writes=R_PROD)
                P.op("dve", lambda e: e.reduce_sum(out=misc[:, 0:2], in_=prod.rearrange("p (k d) -> p k d", k=2), axis=AX.X),
                     reads=R_PROD, writes=rL(4608, 4612))
                P.op("act", lambda e: e.activation(out=misc[:, 2:4], in_=misc[:, 0:2], func=AF.Exp), reads=rL(4608, 4612), writes=rL(4612, 4616))
                P.op("dve", lambda e: e.tensor_tensor(out=misc[:, 4:5], in0=misc[:, 3:4], in1=misc[:, 2:3], op=ALU.subtract),
                     reads=rL(4612, 4616), writes=rL(4616, 4618))
                P.op("dve", lambda e: e.tensor_single_scalar(out=misc[:, 5:6], in_=misc[:, 4:5], scalar=-lambda_init, op=ALU.add),
                     reads=rL(4616, 4618), writes=rL(4618, 4620))
                neglam = misc[:, 5:6]
                r_neglam = rL(4618, 4620)
                P.op("dve", lambda e: e.tensor_single_scalar(out=gain, in_=gain, scalar=(1.0 - lambda_init) * 16.0, op=ALU.mult),
                     reads=R_GAIN, writes=R_GAIN)
                P.dma("sp", "c6", lambda e: e.dma_start(out=oh[0:32, :], in_=consts[0:32, 384:384 + EW]), writes=R_OH)

            def setup_head_pre(h):
                P.op("dve", lambda e: e.tensor_copy(out=lh[0:32, :], in_=tb[0:32, h:h + 1].to_broadcast([32, 128])),
                     reads=rS("tbt"), writes=R_LH)

            def setup_head_a(h):
                P.op("pe", lambda e: e.matmul(pb[6][:, 0:EW], lhsT=lh[0:32, :], rhs=oh[0:32, :], start=True, stop=True),
                     reads=R_LH + R_OH, writes=rPB(6))
                P.op("act", lambda e: e.activation(out=erep, in_=pb[6][:, 0:EW], func=AF.Copy, scale=1.0 / SCALE), reads=rPB(6), writes=rS("ov"))
                P.dma("sp", "es", lambda e: e.dma_start(out=escr[h], in_=erep), reads=rS("ov"), writes=rS("escr", h, h + 1))
                for k in range(2):
                    srcap = bass.AP(escr.tensor, h * 128 * EW + 127 + 128 * k, [[EW - 1, 128], [1, 128]])
                    P.dma("sp", "el%d" % k, lambda e, k=k, srcap=srcap: e.dma_start(out=etmp[:, k, :], in_=srcap),
                          reads=rS("escr", h, h + 1), writes=rL(4640 + k * 256, 4640 + (k + 1) * 256))

            def setup_head_b(h):
                r0, r1 = rL(4640, 4896), rL(4896, 5152)
                P.op("dve", lambda e: e.tensor_tensor(out=etmp[:, 0, :], in0=etmp[:, 0, :], in1=tri_bf[:], op=ALU.mult),
                     reads=r0 + rS("tri_bf"), writes=r0)
                P.op("dve", lambda e: e.tensor_tensor(out=etmp[:, 0, :], in0=etmp[:, 0, :], in1=nmask_bf[:], op=ALU.add),
                     reads=r0 + rS("nmask_bf"), writes=r0)
                for k, rk in ((0, r0), (1, r1)):
                    base = (h * 4 + k * 2) * 128
                    P.op("dve", lambda e, k=k: e.tensor_copy(out=Bt[:, h, k, 0, :], in_=etmp[:, k, :]),
                         reads=rk, writes=rL(base, base + 128))
                    P.op("dve", lambda e, k=k: e.tensor_tensor(out=Bt[:, h, k, 1, :], in0=etmp[:, k, :], in1=Bt[:, h, k, 0, :], op=ALU.subtract),
                         reads=rk + rL(base, base + 128), writes=rL(base + 128, base + 256))

            def setup_head(h):
                setup_head_pre(h)
                setup_head_a(h)
                setup_head_b(h)

            deferred = []
            clock = {"t": 0}
            bank_rr = {"n": 0}
            trbank = {"n": 0}

            seqc = {"p": 0}

            def defer(n, fn, tag="bc", p=0):
                deferred.append([clock["t"] + n, fn, tag, p])

            def flush_tag(tag, pmax=1 << 30):
                k = 0
                while k < len(deferred):
                    if deferred[k][2] == tag and deferred[k][3] <= pmax:
                        deferred.pop(k)[1]()
                    else:
                        k += 1

            def tick(flush=False):
                clock["t"] += 1
                k = 0
                while k < len(deferred):
                    if flush or deferred[k][0] <= clock["t"]:
                        fn = deferred.pop(k)[1]
                        fn()
                    else:
                        k += 1

            def acc_ap(m, k):
                w = 129 if fox else 257
                bk = 3 + 2 * m + k
                return pb[bk][:, 0:w], rPB(bk)

            def early_proj(tc):
                sq, sk = ph["slots0"][0], ph["slots0"][1]
                proj_fm_tc(sq, qT, r_qT, "act", tc)
                proj_fm_tc(sk, kT, r_kT, "dve", tc)
                ph["early"].add(tc)
            ph["early_proj"] = early_proj

            def proj_fm_tc(slot, dstT, r_dst, eng, tc):
                if tc == 3 and carry["fn"] is not None:
                    fn_, carry["fn"] = carry["fn"], None
                    fn_()
                if True:
                    for m in range(2):
                        if True:
                            bk = bank_rr["n"] % 4
                            bank_rr["n"] += 1
                            def g(e, m=m, tc=tc, bk=bk):
                                ins = None
                                for kc in range(KC):
                                    ins = e.matmul(pb[bk][:, :], lhsT=W[slot][:, kc, m * 128:(m + 1) * 128],
                                                   rhs=Xbuf[:, kc, tc * 512:(tc + 1) * 512], start=(kc == 0), stop=(kc == KC - 1))
                                return ins
                            P.op("pe", g, reads=rW(slot) + rX(Xname, 0, KC, tc * 512, (tc + 1) * 512), writes=rPB(bk))
                            if eng == "act":
                                fn = lambda e, m=m, tc=tc, bk=bk: e.activation(out=dstT[:, m, tc * 512:(tc + 1) * 512], in_=pb[bk][:, :], func=AF.Copy)
                            else:
                                fn = lambda e, m=m, tc=tc, bk=bk: e.tensor_copy(out=dstT[:, m, tc * 512:(tc + 1) * 512], in_=pb[bk][:, :])
                            P.op(eng, fn, reads=rPB(bk), writes=r_dst(m, tc * 512, (tc + 1) * 512))
                            tick()

            def do_unit(u, slots):
                sq, sk, sv = slots
                def proj_fm(slot, dstT, r_dst, eng):
                    for tc in range(4):
                        if u == 0 and tc in ph["early"]:
                            continue
                        proj_fm_tc(slot, dstT, r_dst, eng, tc)

                def proj_tm(slot, evac):
                    for bp in range(8):
                        bk = bank_rr["n"] % 4
                        bank_rr["n"] += 1
                        def g(e, bp=bp, bk=bk):
                            ins = None
                            for bb in range(2):
                                b = 2 * bp + bb
                                for kc in range(KC):
                                    ins = e.matmul(pb[bk][:, bb * 256:(bb + 1) * 256], lhsT=Xbuf[:, kc, b * 128:(b + 1) * 128],
                                                   rhs=W[slot][:, kc, :], start=(kc == 0), stop=(kc == KC - 1))
                            return ins
                        P.op("pe", g, reads=rW(slot) + rX(Xname, 0, KC, bp * 256, (bp + 1) * 256), writes=rPB(bk))
                        evac(bp, bk)
                        tick()

                proj_fm(sq, qT, r_qT, "act")
                if fox and u == 0:
                    ph["fox_part2"]()
                sz_slot = load_w(6144 + 256 * u)
                proj_fm(sk, kT, r_kT, "dve")
                nxt = []
                if u + 1 < 8:
                    nxt.append(load_w(256 * (u + 1)))

                def evac_v(bp, bk):
                    if fox:
                        src = pb[bk][:, :].rearrange("p (b m c) -> p b m c", b=2, m=2)
                        dst = V4[:, 2 * bp:2 * bp + 2, :, 0:128]
                    else:
                        src = pb[bk][:, :].rearrange("p (b c) -> p b c", b=2)
                        dst = V3[:, 2 * bp:2 * bp + 2, 0:256]
                    P.op("dve", lambda e: e.tensor_copy(out=dst, in_=src), reads=rPB(bk), writes=r_V(2 * bp, 2 * bp + 2))
                proj_tm(sv, evac_v)
                if u + 1 < 8:
                    nxt.append(load_w(2048 + 256 * (u + 1)))

                def evac_z(bp, bk):
                    src = pb[bk][:, :].rearrange("p (b c) -> p b c", b=2)
                    P.op("act", lambda e: e.activation(out=sz[:, 2 * bp:2 * bp + 2, :], in_=src, func=AF.Silu),
                         reads=rPB(bk), writes=r_sz(2 * bp, 2 * bp + 2))
                proj_tm(sz_slot, evac_z)
                if u + 1 < 8:
                    nxt.append(load_w(4096 + 256 * (u + 1)))
                else:
                    for q4 in range(4):
                        P.dma("pool", "wo%d" % q4,
                              lambda e, q4=q4: e.dma_start(out=Xbuf[:, 4 * q4:4 * q4 + 4, :], in_=wov[:, 4 * q4:4 * q4 + 4, :]),
                              writes=rX(Xname, 4 * q4, 4 * q4 + 4, 0, S))

                if fox:
                    bsel = bias2[u % 2]
                    boff = 2336 + (u % 2) * 1024
                    for m in range(2):
                        h = 2 * u + m
                        P.op("dve", lambda e, m=m, h=h: e.tensor_tensor(
                            out=bsel[:, m, :, :], in0=cneg[:, :, h:h + 1].to_broadcast([128, 16, 16]),
                            in1=crefn[:, :, h].unsqueeze(1).to_broadcast([128, 16, 16]), op=ALU.subtract),
                            reads=rL(1312, 2336), writes=rL(boff + m * 512, boff + (m + 1) * 512))
                tiles = []
                pmap = {}
                for I in (0, 7, 1, 6, 2, 5, 3, 4):
                    pmap[I] = seqc["p"]
                    seqc["p"] += 1
                    for m in range(2):
                        for j in range(2 * I + 2):
                            tiles.append((I, m, j))
                T = len(tiles)

                def geom(t):
                    I, m, j = tiles[t]
                    s = t % 4
                    i0 = 2 * I if j <= 2 * I else 2 * I + 1
                    nblk = 2 * I + 2 - i0
                    return I, m, j, s, i0, nblk, t % 3, 0

                def qk(t):
                    I, m, j, s, i0, nblk, bank, off = geom(t)
                    w = nblk * 128
                    q0 = i0 * 128
                    extra = []
                    for k in range(nblk):
                        d = i0 + k - j
                        if fox and d == 0:
                            extra.append((k, nmask_bf[:], rS("nmask_bf")))
                        elif (not fox) and d in (0, 1):
                            for l in range(2):
                                base = (u * 4 + d * 2 + l) * 128
                                extra.append((k, Bt[:, u, d, l, :], rL(base, base + 128)))
                    def g(e):
                        ins = e.matmul(pb[bank][:, 0:w], lhsT=kT[:, m, j * 128:(j + 1) * 128], rhs=qT[:, m, q0:q0 + w],
                                       start=True, stop=(len(extra) == 0))
                        for n, (k, rhs, _) in enumerate(extra):
                            ins = e.matmul(pb[bank][:, k * 128:(k + 1) * 128], lhsT=ident[:], rhs=rhs, start=False, stop=(n == len(extra) - 1))
                        return ins
                    rd = r_kT(m, j * 128, (j + 1) * 128) + r_qT(m, q0, q0 + w)
                    for _, _, r in extra:
                        rd = rd + r
                    if extra:
                        rd = rd + rS("ident")
                    P.op("pe", g, reads=rd, writes=rPB(bank), sig=True)

                def r_Pblk(s, k):
                    return rU(PT_OFF + s * 256 + k * 128, PT_OFF + s * 256 + (k + 1) * 128)

                def ex(t):
                    I, m, j, s, i0, nblk, bank, off = geom(t)
                    w = nblk * 128
                    src = pb[bank][:, 0:w]
                    if fox:
                        P.op("act", lambda e: e.activation(out=Pt[s][:, 0:w], in_=src, func=AF.Exp,
                                                           bias=bsel[:, m, j, 2 * I:2 * I + 1], scale=SCALE),
                             reads=rPB(bank) + rL(boff + m * 512, boff + (m + 1) * 512),
                             writes=rU(PT_OFF + s * 256, PT_OFF + s * 256 + w))
                    else:
                        P.op("act", lambda e: e.activation(out=Pt[s][:, 0:w], in_=src, func=AF.Exp, scale=SCALE),
                             reads=rPB(bank), writes=rU(PT_OFF + s * 256, PT_OFF + s * 256 + w))

                def pv(t):
                    I, m, j, s, i0, nblk, bank, off = geom(t)
                    for k in range(nblk):
                        i = i0 + k
                        out, r_out = acc_ap(m, i - 2 * I)
                        rhs = V4[:, j, m, :] if fox else V3[:, j, 0:257]
                        P.op("pe", lambda e, k=k, out=out, rhs=rhs, i=i: e.matmul(out, lhsT=Pt[s][:, k * 128:(k + 1) * 128], rhs=rhs,
                                                                                   start=(j == 0), stop=(j == i)),
                             reads=r_Pblk(s, k) + r_V(j, j + 1), writes=r_out)
                    if j == 2 * I + 1:
                        finalize(I, m)

                def emit_transposes(I, gt, gname, c0, ncol):
                    tb_ = 7
                    def tr(e):
                        ins = None
                        for c in range(ncol):
                            for k in range(2):
                                ins = e.transpose(pb_bf[tb_][:, (c * 2 + k) * 128:(c * 2 + k + 1) * 128],
                                                  gt[:, k, (c0 % 2 + c) * 128:(c0 % 2 + c + 1) * 128] if fox else gt[:, k, c * 128:(c + 1) * 128],
                                                  ident[:])
                        return ins
                    if fox:
                        rd = [(gname, (k * 256 + (c0 % 2) * 128) * 2, (k * 256 + (c0 % 2 + 1) * 128) * 2) for k in range(2)]
                    else:
                        rd = rS(gname)
                    P.op("pe", tr, reads=rd + rS("ident"), writes=rPB(tb_, 0, ncol * 512))
                    P.op("dve", lambda e: e.tensor_copy(out=Gbuf[:, c0:c0 + ncol, 256 * I:256 * I + 256],
                                                        in_=pb_bf[tb_][:, 0:ncol * 256].rearrange("p (c t) -> p c t", c=ncol)),
                         reads=rPB(tb_, 0, ncol * 512), writes=rX(Gname, c0, c0 + ncol, 256 * I, 256 * I + 256))

                def finalize(I, m):
                    p = pmap[I]
                    gt = gtok[p % 2]
                    gname = "gtok%d" % (p % 2)
                    if fox:
                        flush_tag("tr", p - 2)
                        ro = (p % 2) * 8 + m * 2
                        for k in range(2):
                            blk = 2 * I + k
                            a, r_a = acc_ap(m, k)
                            P.op("dve", lambda e, k=k, a=a: e.reciprocal(out=rin[:, ro + k:ro + k + 1], in_=a[:, 128:129]),
                                 reads=r_a, writes=rS("rin", ro + k, ro + k + 1))
                            P.op("dve", lambda e, k=k, a=a, blk=blk: e.scalar_tensor_tensor(
                                out=gt[:, k, m * 128:(m + 1) * 128], in0=a[:, 0:128], scalar=rin[:, ro + k:ro + k + 1],
                                in1=sz[:, blk, m * 128:(m + 1) * 128], op0=ALU.mult, op1=ALU.mult),
                                reads=r_a + rS("rin", ro + k, ro + k + 1) + r_sz(blk, blk + 1),
                                writes=rS(gname, (k * 256 + m * 128) * 2, (k * 256 + (m + 1) * 128) * 2))
                        defer(10, lambda: emit_transposes(I, gt, gname, 2 * u + m, 1), "tr", p)
                    else:
                        ro = (p % 2) * 8
                        ov = ptmp_o
                        if m == 0:
                            for k in range(2):
                                a, r_a = acc_ap(0, k)
                                P.op("dve", lambda e, k=k, a=a: e.reciprocal(out=rin[:, ro + k:ro + k + 1], in_=a[:, 256:257]),
                                     reads=r_a, writes=rS("rin", ro + k, ro + k + 1))
                                P.op("dve", lambda e, k=k, a=a: e.tensor_scalar_mul(out=o0n[:, k, :], in0=a[:, 0:256], scalar1=rin[:, ro + k:ro + k + 1]),
                                     reads=r_a + rS("rin", ro + k, ro + k + 1), writes=rL(4640 + k * 512, 4640 + (k + 1) * 512))
                        else:
                            flush_tag("bc")
                            for k in range(2):
                                a, r_a = acc_ap(1, k)
                                P.op("dve", lambda e, k=k, a=a: e.reciprocal(out=rin[:, ro + 2 + k:ro + 3 + k], in_=a[:, 256:257]),
                                     reads=r_a, writes=rS("rin", ro + 2 + k, ro + 3 + k))
                                P.op("dve", lambda e, k=k, a=a: e.tensor_copy(out=ov[:, k, :], in_=a[:, 0:256]),
                                     reads=r_a, writes=rS("ov", k * 1024, (k + 1) * 1024))
                            for k in range(2):
                                P.op("dve", lambda e, k=k: e.tensor_tensor(out=rin[:, ro + 4 + k:ro + 5 + k], in0=rin[:, ro + 2 + k:ro + 3 + k],
                                                                          in1=neglam, op=ALU.mult),
                                     reads=rS("rin", ro + 2 + k, ro + 3 + k) + r_neglam, writes=rS("rin", ro + 4 + k, ro + 5 + k))
                                P.op("dve", lambda e, k=k: e.scalar_tensor_tensor(out=ov[:, k, :], in0=ov[:, k, :], scalar=rin[:, ro + 4 + k:ro + 5 + k],
                                                                                  in1=o0n[:, k, :], op0=ALU.mult, op1=ALU.add),
                                     reads=rS("ov", k * 1024, (k + 1) * 1024) + rS("rin", ro + 4 + k, ro + 5 + k) + rL(4640 + k * 512, 4640 + (k + 1) * 512),
                                     writes=rS("ov", k * 1024, (k + 1) * 1024))
                            for k in range(2):
                                P.op("dve", lambda e, k=k: e.scalar_tensor_tensor(out=sqj[:, :], in0=ov[:, k, :], scalar=1.0, in1=ov[:, k, :],
                                                                                  op0=ALU.mult, op1=ALU.mult,
                                                                                  accum_out=rin[:, ro + 6 + k:ro + 7 + k]),
                                     reads=rS("ov", k * 1024, (k + 1) * 1024), writes=rL(6432, 6944) + rS("rin", ro + 6 + k, ro + 7 + k))
                            P.op("pool", lambda e: e.tensor_single_scalar(out=rin[:, ro + 6:ro + 8], in_=rin[:, ro + 6:ro + 8],
                                                                          scalar=256.0 * RMS_EPS, op=ALU.add),
                                 reads=rS("rin", ro + 6, ro + 8), writes=rS("rin", ro + 6, ro + 8))
                            P.op("pool", lambda e: e.tensor_tensor(out=rin[:, ro + 6:ro + 8], in0=rin[:, ro + 6:ro + 8], in1=lnst[:, 10:12], op=ALU.pow),
                                 reads=rS("rin", ro + 6, ro + 8) + rS("lnst", 10, 12), writes=rS("rin", ro + 6, ro + 8))
                            def stage_c():
                                flush_tag("tr", p - 2)
                                for k in range(2):
                                    blk = 2 * I + k
                                    P.op("dve", lambda e, k=k: e.scalar_tensor_tensor(out=ov[:, k, :], in0=ov[:, k, :], scalar=rin[:, ro + 6 + k:ro + 7 + k],
                                                                                      in1=gain, op0=ALU.mult, op1=ALU.mult),
                                         reads=rS("ov", k * 1024, (k + 1) * 1024) + rS("rin", ro + 6 + k, ro + 7 + k) + R_GAIN,
                                         writes=rS("ov", k * 1024, (k + 1) * 1024))
                                    P.op("dve", lambda e, k=k, blk=blk: e.tensor_tensor(out=gt[:, k, :], in0=ov[:, k, :], in1=sz[:, blk, :], op=ALU.mult),
                                         reads=rS("ov", k * 1024, (k + 1) * 1024) + r_sz(blk, blk + 1), writes=rS(gname, k * 512, (k + 1) * 512))
                            defer(13, stage_c, "bc", p)
                            defer(30, lambda: emit_transposes(I, gt, gname, 2 * u, 2), "tr", p)

                for t in range(T + 2):
                    if t < T:
                        qk(t)
                    if 0 <= t - 1 < T:
                        ex(t - 1)
                    if t - 2 >= 0:
                        pv(t - 2)
                    tick()
                return nxt

            def units():
                if fox:
                    P.op("pool", lambda e: e.memset(V4[:, :, :, 128:129], 1.0), writes=r_V(0, 16))
                else:
                    P.op("pool", lambda e: e.memset(V3[:, :, 256:257], 1.0), writes=r_V(0, 16))
                slots = ph["slots0"]
                for u in range(8):
                    slots = do_unit(u, slots)
                tick(flush=True)

            def epilogue(per_block=None, pre_block=None):
                Wo = Xbuf
                P.dma("sp", "lg", lambda e: e.dma_start(out=lng, in_=ln_g[li:li + 1, :].partition_broadcast(128)), writes=r_lng)
                P.dma("sp", "lb", lambda e: e.dma_start(out=lnb, in_=ln_b[li:li + 1, :].partition_broadcast(128)), writes=r_lnb)
                out_dmas = []

                def load_res(b):
                    s = b % 2
                    rd = [(resid_reg, b, b + 1)] if resid_reg else []
                    P.dma("sp", "xr%d" % s, lambda e: e.dma_start(out=xres[s], in_=resid_src[b * 128:(b + 1) * 128, :]),
                          reads=rd, writes=r_xres[s])

                def epi_y(b):
                    s = b % 2
                    xr = xres[s]
                    so = s * 16
                    for p4 in range(4):
                        def g(e, p4=p4):
                            ins = None
                            for kc in range(KC):
                                ins = e.matmul(pb[p4][:, :], lhsT=Gbuf[:, kc, b * 128:(b + 1) * 128], rhs=Wo[:, kc, p4 * 512:(p4 + 1) * 512],
                                               start=(kc == 0), stop=(kc == KC - 1))
                            return ins
                        P.op("pe", g, reads=rX(Gname, 0, KC, b * 128, (b + 1) * 128) + rX(Xname, 0, KC, p4 * 512, (p4 + 1) * 512), writes=rPB(p4))
                        lo, hi = 8192 + s * 4096 + p4 * 1024, 8192 + s * 4096 + (p4 + 1) * 1024
                        P.op("dve", lambda e, p4=p4: e.scalar_tensor_tensor(out=xr[:, p4 * 512:(p4 + 1) * 512], in0=xr[:, p4 * 512:(p4 + 1) * 512],
                                                                            scalar=float(ALPHA), in1=pb[p4][:, :], op0=ALU.mult, op1=ALU.add,
                                                                            accum_out=lnst[:, so + p4:so + p4 + 1]),
                             reads=rU(lo, hi) + rPB(p4), writes=rU(lo, hi) + rS("lnst", so + p4, so + p4 + 1))
                    if b + 1 < NB:
                        load_res(b + 1)

                def epi_ln(b):
                    s = b % 2
                    xr = xres[s]
                    so = s * 16
                    c = lambda k: lnst[:, so + k:so + k + 1]
                    rc = lambda k0, k1: rS("lnst", so + k0, so + k1)
                    P.op("act", lambda e: e.activation(out=xbf, in_=xr, func=AF.Square, accum_out=c(6)),
                         reads=r_xres[s], writes=r_xbf + rc(6, 7))
                    P.op("dve", lambda e: e.reduce_sum(out=c(4), in_=lnst[:, so:so + 4], axis=AX.X), reads=rc(0, 4), writes=rc(4, 5))
                    P.op("dve", lambda e: e.tensor_single_scalar(out=c(5), in_=c(4), scalar=-1.0 / D, op=ALU.mult), reads=rc(4, 5), writes=rc(5, 6))
                    P.op("dve", lambda e: e.tensor_tensor(out=c(7), in0=c(5), in1=c(5), op=ALU.mult), reads=rc(5, 6), writes=rc(7, 8))
                    P.op("dve", lambda e: e.scalar_tensor_tensor(out=c(8), in0=c(6), scalar=1.0 / D, in1=c(7), op0=ALU.mult, op1=ALU.subtract),
                         reads=rc(6, 8), writes=rc(8, 9))
                    P.op("act", lambda e: e.activation(out=c(9), in_=c(8), func=AF.Ln, bias=LN_EPS), reads=rc(8, 9), writes=rc(9, 10))
                    P.op("act", lambda e: e.activation(out=c(9), in_=c(9), func=AF.Exp, scale=-0.5), reads=rc(9, 10), writes=rc(9, 10))
                    P.op("dve", lambda e: e.scalar_tensor_tensor(out=xr, in0=xr, scalar=c(5), in1=lng, op0=ALU.add, op1=ALU.mult),
                         reads=r_xres[s] + rc(5, 6) + r_lng, writes=r_xres[s])
                    P.op("dve", lambda e: e.scalar_tensor_tensor(out=xr, in0=xr, scalar=c(9), in1=lnb, op0=ALU.mult, op1=ALU.add),
                         reads=r_xres[s] + rc(9, 10) + r_lnb, writes=r_xres[s])
                    wr = [(out_reg, b, b + 1)] if out_reg else []
                    out_dmas.append(P.dma("sp", "xo%d" % s, lambda e: e.dma_start(out=out_dst[b * 128:(b + 1) * 128, :], in_=xr),
                                          reads=r_xres[s], writes=wr))
                    if make_next:
                        P.op("act", lambda e: e.activation(out=xbf, in_=xr, func=AF.Copy), reads=r_xres[s], writes=r_xbf)

                def epi_tr(b):
                    banks = (4, 5)
                    def tr(e):
                        ins = None
                        for c in range(KC):
                            o = pb_bf[banks[c // 8]][:, (c % 8) * 128:(c % 8 + 1) * 128]
                            ins = e.transpose(o, xbf[:, c * 128:(c + 1) * 128], ident[:])
                        return ins
                    P.op("pe", tr, reads=r_xbf + rS("ident"), writes=rPB(banks[0]) + rPB(banks[1]))
                    for hf in range(2):
                        src = pb_bf[banks[hf]].rearrange("p (c t) -> p c t", c=8)
                        dst = Gbuf[:, hf * 8:(hf + 1) * 8, b * 128:(b + 1) * 128]
                        if hf == 0:
                            fn = lambda e, src=src, dst=dst: e.tensor_copy(out=dst, in_=src)
                        else:
                            fn = lambda e, src=src, dst=dst: e.activation(out=dst, in_=src, func=AF.Copy)
                        P.op("dve" if hf == 0 else "act", fn, reads=rPB(banks[hf]),
                             writes=rX(Gname, hf * 8, hf * 8 + 8, b * 128, (b + 1) * 128))

                load_res(0)
                for b in range(NB):
                    if pre_block is not None:
                        pre_block(b)
                    epi_y(b)
                    if make_next and b >= 1:
                        epi_tr(b - 1)
                    if per_block is not None:
                        per_block(b)
                    epi_ln(b)
                if make_next:
                    carry["fn"] = lambda: epi_tr(NB - 1)
                return out_dmas

            ph["setup_fox"] = setup_fox if fox else (lambda: None)
            ph["setup_head"] = (lambda h: None) if fox else setup_head
            ph["setup_head_a"] = (lambda h: None) if fox else setup_head_a
            ph["setup_head_pre"] = (lambda h: None) if fox else setup_head_pre
            ph["setup_head_b"] = (lambda h: None) if fox else setup_head_b
            ph["units"] = units
            ph["epilogue"] = epilogue
            return ph

        ptmp_o = sb("ov", [128, 2, 256], F32)

        bufs = [(XA, "XA", XB, "XB"), (XB, "XB", XA, "XA")]
        specs = []
        for n, li in enumerate(layers):
            Xb, Xn, Gb, Gn = bufs[n % 2]
            first = (n == 0)
            final = (n == len(layers) - 1)
            specs.append((li, Xb, Xn, Gb, Gn, x_in if first else x1d, None if first else "x1d",
                          y_out if final else x1d, None if final else "x1d", not final))
        last = []
        cur = make_layer(*specs[0])
        def pro_hook(b, cur=cur):
            if specs[0][0] != 0:
                return
            if b == 3:
                cur["load_wf"]()
            if b >= 8:
                cur["fproj_block"](2 * (b - 8))
                cur["fproj_block"](2 * (b - 8) + 1)
        prologue(XA, "XA", pro_hook)
        for i in range(3):
            cur["init_load"](i, 4)
        if specs[0][0] == 1:
            for h in range(8):
                cur["setup_head"](h)
        for n in range(len(specs)):
            cur["setup_fox"]()
            cur["units"]()
            nxt = None
            hook = None
            if n + 1 < len(specs):
                nxt = make_layer(*specs[n + 1])
                for i in range(3):
                    nxt["init_load"](i)
                def hook(b, nxt=nxt):
                    if 1 <= b <= 8:
                        nxt["setup_head_b"](b - 1)
                    if b < 8:
                        nxt["setup_head_a"](b)
            pre = (lambda b, nxt=nxt: nxt["setup_head_pre"](b) if b < 8 else None) if nxt is not None else None
            last = cur["epilogue"](hook, pre)
            cur = nxt
        P.wait_all("sp", last)
        P.emit()
    return nc


_CONSTS = None


def _in_map(layers, xb, inp):
    global _CONSTS
    if _CONSTS is None:
        _CONSTS = _consts()
    m = {"x": np.ascontiguousarray(xb, dtype=np.float32), "consts": _CONSTS,
         "ln_g": np.ascontiguousarray(inp["ln_g"], dtype=np.float32), "ln_b": np.ascontiguousarray(inp["ln_b"], dtype=np.float32)}
    if 0 in layers:
        m["w_in0"] = np.ascontiguousarray(inp["fox_w_in"][0]); m["w_out0"] = np.ascontiguousarray(inp["fox_w_out"][0])
        m["b_f"] = np.ascontiguousarray(inp["fox_b_f"][0:1])
    if 1 in layers:
        m["w_in1"] = np.ascontiguousarray(inp["diff_w_in"][0]); m["w_out1"] = np.ascontiguousarray(inp["diff_w_out"][0])
        m["lam_q"] = np.ascontiguousarray(inp["diff_lam_q"][0].reshape(1, 256)); m["lam_k"] = np.ascontiguousarray(inp["diff_lam_k"][0].reshape(1, 256))
        m["subln_g"] = np.ascontiguousarray(inp["diff_subln_g"][0:1]); m["rel_bias"] = np.ascontiguousarray(inp["rel_bias"])
    return m


_PROGS = {}


def run_layers(layers, x, inp, trace=False):
    key = tuple(layers)
    if key not in _PROGS:
        _PROGS[key] = build_program(list(layers))
    nc = _PROGS[key]
    in_maps = [_in_map(layers, x[b], inp) for b in range(8)]
    res = run_bass_kernel_spmd(nc, in_maps, core_ids=list(range(8)))
    return np.stack([r["y"] for r in res.results], axis=0)


MODE = "fused"


def kernel(x, fox_w_in, fox_b_f, fox_w_out, diff_w_in, diff_lam_q, diff_lam_k, diff_subln_g, diff_w_out,
           rel_bias, ln_g, ln_b):
    inp = dict(fox_w_in=np.asarray(fox_w_in), fox_b_f=np.asarray(fox_b_f), fox_w_out=np.asarray(fox_w_out),
               diff_w_in=np.asarray(diff_w_in), diff_lam_q=np.asarray(diff_lam_q), diff_lam_k=np.asarray(diff_lam_k),
               diff_subln_g=np.asarray(diff_subln_g), diff_w_out=np.asarray(diff_w_out), rel_bias=np.asarray(rel_bias),
               ln_g=np.asarray(ln_g), ln_b=np.asarray(ln_b))
    x = np.asarray(x, dtype=np.float32)
    if MODE == "fused":
        return run_layers([0, 1], x, inp).astype(np.float32)
    x1 = run_layers([0], x, inp)
    return run_layers([1], x1, inp).astype(np.float32)
```
